# Optimizing a Trainium2 kernel written in Bass

```python
import jax, jax.numpy as jnp
from jax import lax
import numpy as np

D_MODEL = 1024
BATCH = 8
SEQ = 4096
DEPTH = 4

EPS = 1e-6
N_EVEN = (DEPTH + 1) // 2
N_ODD = DEPTH // 2
CONV_WIDTH = D_MODEL // 2
CONV_KERNEL = 31
MLSTM_HEADS = 4
MLSTM_HEAD_DIM = (D_MODEL // 2) // MLSTM_HEADS
MLSTM_WIDTH = MLSTM_HEADS * MLSTM_HEAD_DIM
MLSTM_QK_CONV = 4
MLSTM_CHUNK = 64
EVEN_IN = 2 * CONV_WIDTH + 4 * MLSTM_WIDTH + 2 * MLSTM_HEADS
GLA_HEADS = 4
GLA_KEY_DIM = (D_MODEL // 2) // GLA_HEADS
GLA_VAL_DIM = D_MODEL // GLA_HEADS
GLA_KW = GLA_HEADS * GLA_KEY_DIM
GLA_VW = GLA_HEADS * GLA_VAL_DIM
GLA_GATE_RANK = 16
GLA_TAU = 16.0
GLA_CHUNK = 64
ODD_IN = 2 * GLA_KW + 2 * GLA_VW + GLA_GATE_RANK
FFN_HIDDEN = -(-8 * D_MODEL // (3 * 256)) * 256

kernel_name = "hybrid_conv_mlstm_gla_trunk"


def rmsnorm(x, g):
    xf = x.astype(jnp.float32)
    y = xf * lax.rsqrt(jnp.mean(xf * xf, -1, keepdims=True) + EPS)
    return (y * g.astype(jnp.float32)).astype(x.dtype)


def layernorm(x, g, b):
    xf = x.astype(jnp.float32)
    mu = jnp.mean(xf, -1, keepdims=True)
    xc = xf - mu
    y = xc * lax.rsqrt(jnp.mean(xc * xc, -1, keepdims=True) + EPS)
    return (y * g.astype(jnp.float32) + b.astype(jnp.float32)).astype(x.dtype)


def causal_depthwise_conv(x, w, b):
    width = w.shape[0]
    out = lax.conv_general_dilated(
        x, w[:, None, :].astype(x.dtype), window_strides=(1,), padding=[(width - 1, 0)],
        dimension_numbers=("NWC", "WIO", "NWC"), feature_group_count=x.shape[-1])
    return out + b.astype(x.dtype)


def to_chunks(t, L):
    B, H, T = t.shape[:3]
    t = t.reshape((B, H, T // L, L) + t.shape[3:])
    return jnp.moveaxis(t, 2, 0)


def from_chunks(t):
    t = jnp.moveaxis(t, 0, 2)
    B, H, nc, L = t.shape[:4]
    return t.reshape((B, H, nc * L) + t.shape[4:])


def mlstm_chunkwise(q, k, v, i_pre, f_pre):
    B, H, T, Dh = q.shape
    L = MLSTM_CHUNK
    q = q.astype(jnp.float32)
    k = k.astype(jnp.float32) * (Dh ** -0.5)
    v = v.astype(jnp.float32)
    logf = jax.nn.log_sigmoid(f_pre.astype(jnp.float32))
    ig = i_pre.astype(jnp.float32)
    causal = jnp.tril(jnp.ones((L, L), dtype=bool))

    def step(carry, inp):
        C, n, m = carry
        qc, kc, vc, ic, lfc = inp
        b = jnp.cumsum(lfc, -1)
        a = b[..., -1]
        logD = jnp.where(causal, b[..., :, None] - b[..., None, :] + ic[..., None, :], -jnp.inf)
        m_inter = b + m[..., None]
        m_t = jnp.maximum(m_inter, jnp.max(logD, -1))
        S = jnp.einsum('bhld,bhsd->bhls', qc, kc) * jnp.exp(logD - m_t[..., None])
        w_inter = jnp.exp(m_inter - m_t)
        num = jnp.einsum('bhls,bhse->bhle', S, vc) + w_inter[..., None] * jnp.einsum('bhld,bhde->bhle', qc, C)
        den = jnp.sum(S, -1) + w_inter * jnp.einsum('bhld,bhd->bhl', qc, n)
        h = num / jnp.maximum(jnp.abs(den), jnp.exp(-m_t))[..., None]
        g = a[..., None] - b + ic
        m_new = jnp.maximum(a + m, jnp.max(g, -1))
        decay = jnp.exp(a + m - m_new)
        wk = jnp.exp(g - m_new[..., None])[..., None] * kc
        C = decay[..., None, None] * C + jnp.einsum('bhsd,bhse->bhde', wk, vc)
        n = decay[..., None] * n + jnp.sum(wk, -2)
        return (C, n, m_new), h

    init = (jnp.zeros((B, H, Dh, Dh), jnp.float32), jnp.zeros((B, H, Dh), jnp.float32),
            jnp.zeros((B, H), jnp.float32))
    _, hs = lax.scan(step, init, (to_chunks(q, L), to_chunks(k, L), to_chunks(v, L),
                                  to_chunks(ig, L), to_chunks(logf, L)))
    return from_chunks(hs)


def gla_chunkwise(q, k, v, log_alpha):
    B, H, T, Dk = q.shape
    Dv = v.shape[-1]
    L = GLA_CHUNK
    q = q.astype(jnp.float32) * (Dk ** -0.5)
    k = k.astype(jnp.float32)
    v = v.astype(jnp.float32)
    causal = jnp.tril(jnp.ones((L, L), dtype=bool))

    def step(S, inp):
        qc, kc, vc, lac = inp
        b = jnp.cumsum(lac, axis=-2)
        a = b[..., -1:, :]
        diff = b[..., :, None, :] - b[..., None, :, :]
        decay = jnp.exp(jnp.where(causal[..., None], diff, -jnp.inf))
        A = jnp.einsum('bhld,bhsd,bhlsd->bhls', qc, kc, decay)
        o = jnp.einsum('bhls,bhse->bhle', A, vc) + jnp.einsum('bhld,bhde->bhle', qc * jnp.exp(b), S)
        S = jnp.exp(a[..., 0, :])[..., None] * S + jnp.einsum('bhsd,bhse->bhde', kc * jnp.exp(a - b), vc)
        return S, o

    _, os = lax.scan(step, jnp.zeros((B, H, Dk, Dv), jnp.float32),
                     (to_chunks(q, L), to_chunks(k, L), to_chunks(v, L),
                      to_chunks(log_alpha.astype(jnp.float32), L)))
    return from_chunks(os)


def split_heads(t, H):
    B, T, W = t.shape
    return t.reshape(B, T, H, W // H).transpose(0, 2, 1, 3)


def head_rmsnorm(t, g):
    B, T, H, D = t.shape
    y = t * lax.rsqrt(jnp.mean(t * t, -1, keepdims=True) + EPS)
    return y.reshape(B, T, H * D) * g.astype(jnp.float32)


def conv_mlstm_mixer(h, w_in, conv_w, conv_b, ln_g, ln_b, qk_conv_w, qk_conv_b, gate_b, head_g, w_out):
    B, T, _ = h.shape
    H, Dh = MLSTM_HEADS, MLSTM_HEAD_DIM
    z = h @ w_in
    cuts = [CONV_WIDTH, 2 * CONV_WIDTH] + [2 * CONV_WIDTH + j * MLSTM_WIDTH for j in range(1, 5)]
    ca, cg, q, k, v, og, gates = jnp.split(z, cuts, -1)
    u = ca * jax.nn.sigmoid(cg)
    u = causal_depthwise_conv(u, conv_w, conv_b)
    u = jax.nn.silu(layernorm(u, ln_g, ln_b))
    qk = jax.nn.silu(causal_depthwise_conv(jnp.concatenate([q, k], -1), qk_conv_w, qk_conv_b))
    q, k = jnp.split(qk, 2, -1)
    g = (gates.astype(jnp.float32) + gate_b.astype(jnp.float32)).transpose(0, 2, 1)
    hm = mlstm_chunkwise(split_heads(q, H), split_heads(k, H), split_heads(v, H), g[:, :H], g[:, H:])
    hm = hm.transpose(0, 2, 1, 3) * jax.nn.sigmoid(og.astype(jnp.float32)).reshape(B, T, H, Dh)
    hm = head_rmsnorm(hm, head_g).astype(h.dtype)
    return jnp.concatenate([u, hm], -1) @ w_out


def gla_mixer(h, w_in, gate_w2, gate_b, head_g, w_out):
    B, T, _ = h.shape
    H = GLA_HEADS
    z = h @ w_in
    q, k, v, r, gl = jnp.split(z, [GLA_KW, 2 * GLA_KW, 2 * GLA_KW + GLA_VW, 2 * GLA_KW + 2 * GLA_VW], -1)
    log_alpha = jax.nn.log_sigmoid((gl @ gate_w2 + gate_b).astype(jnp.float32)) / GLA_TAU
    o = gla_chunkwise(split_heads(q, H), split_heads(k, H), split_heads(v, H), split_heads(log_alpha, H))
    o = head_rmsnorm(o.transpose(0, 2, 1, 3), head_g) * jax.nn.silu(r.astype(jnp.float32))
    return o.astype(h.dtype) @ w_out


def swiglu(h, w1, w3, w2):
    return (jax.nn.silu(h @ w1) * (h @ w3)) @ w2


def setup_inputs(seed: int = 0) -> dict:
    key = jax.random.key(seed)
    ks = jax.random.split(key, 24)
    f32 = jnp.float32

    def nrm(k, shape, scale):
        return jax.random.normal(k, shape, f32) * scale

    forget_bias = jnp.linspace(3.0, 6.0, MLSTM_HEADS, dtype=f32)[None, :] + nrm(ks[9], (N_EVEN, MLSTM_HEADS), 0.1)
    input_bias = nrm(ks[10], (N_EVEN, MLSTM_HEADS), 0.1)
    return {
        "x": nrm(ks[0], (BATCH, SEQ, D_MODEL), 1.0),
        "mix_norm_g": 1.0 + nrm(ks[1], (DEPTH, D_MODEL), 0.02),
        "ffn_norm_g": 1.0 + nrm(ks[2], (DEPTH, D_MODEL), 0.02),
        "ffn_w1": nrm(ks[3], (DEPTH, D_MODEL, FFN_HIDDEN), D_MODEL ** -0.5),
        "ffn_w3": nrm(ks[4], (DEPTH, D_MODEL, FFN_HIDDEN), D_MODEL ** -0.5),
        "ffn_w2": nrm(ks[5], (DEPTH, FFN_HIDDEN, D_MODEL), FFN_HIDDEN ** -0.5),
        "ev_w_in": nrm(ks[6], (N_EVEN, D_MODEL, EVEN_IN), D_MODEL ** -0.5),
        "ev_conv_w": nrm(ks[7], (N_EVEN, CONV_KERNEL, CONV_WIDTH), CONV_KERNEL ** -0.5),
        "ev_conv_b": nrm(ks[8], (N_EVEN, CONV_WIDTH), 0.02),
        "ev_ln_g": 1.0 + nrm(ks[11], (N_EVEN, CONV_WIDTH), 0.02),
        "ev_ln_b": nrm(ks[12], (N_EVEN, CONV_WIDTH), 0.02),
        "ev_qk_conv_w": nrm(ks[13], (N_EVEN, MLSTM_QK_CONV, 2 * MLSTM_WIDTH), MLSTM_QK_CONV ** -0.5),
        "ev_qk_conv_b": nrm(ks[14], (N_EVEN, 2 * MLSTM_WIDTH), 0.02),
        "ev_gate_b": jnp.concatenate([input_bias, forget_bias], -1),
        "ev_head_g": 1.0 + nrm(ks[15], (N_EVEN, MLSTM_WIDTH), 0.02),
        "ev_w_out": nrm(ks[16], (N_EVEN, CONV_WIDTH + MLSTM_WIDTH, D_MODEL), (CONV_WIDTH + MLSTM_WIDTH) ** -0.5),
        "od_w_in": nrm(ks[17], (N_ODD, D_MODEL, ODD_IN), D_MODEL ** -0.5),
        "od_gate_w2": nrm(ks[18], (N_ODD, GLA_GATE_RANK, GLA_KW), GLA_GATE_RANK ** -0.5),
        "od_gate_b": nrm(ks[19], (N_ODD, GLA_KW), 0.1),
        "od_head_g": 1.0 + nrm(ks[20], (N_ODD, GLA_VW), 0.02),
        "od_w_out": nrm(ks[21], (N_ODD, GLA_VW, D_MODEL), GLA_VW ** -0.5),
        "final_norm_g": 1.0 + nrm(ks[22], (D_MODEL,), 0.02),
    }


def reference(x, mix_norm_g, ffn_norm_g, ffn_w1, ffn_w3, ffn_w2,
              ev_w_in, ev_conv_w, ev_conv_b, ev_ln_g, ev_ln_b, ev_qk_conv_w, ev_qk_conv_b,
              ev_gate_b, ev_head_g, ev_w_out,
              od_w_in, od_gate_w2, od_gate_b, od_head_g, od_w_out, final_norm_g):
    for l in range(DEPTH):
        h = rmsnorm(x, mix_norm_g[l])
        if l % 2 == 0:
            e = l // 2
            x = x + conv_mlstm_mixer(h, ev_w_in[e], ev_conv_w[e], ev_conv_b[e], ev_ln_g[e], ev_ln_b[e],
                                     ev_qk_conv_w[e], ev_qk_conv_b[e], ev_gate_b[e], ev_head_g[e], ev_w_out[e])
        else:
            o = l // 2
            x = x + gla_mixer(h, od_w_in[o], od_gate_w2[o], od_gate_b[o], od_head_g[o], od_w_out[o])
        h = rmsnorm(x, ffn_norm_g[l])
        x = x + swiglu(h, ffn_w1[l], ffn_w3[l], ffn_w2[l])
    return rmsnorm(x, final_norm_g)
```

```python
import math
import numpy as np
import concourse.bass as bass
import concourse.mybir as mybir
from concourse.bass_utils import run_bass_kernel_spmd

F32 = mybir.dt.float32
BF16 = mybir.dt.bfloat16
ALU = mybir.AluOpType
AF = mybir.ActivationFunctionType

D = 1024
KC = 8
TT = 512
NS = 4
FH = 2816
FJ = 22
EPS = 1e-6
SAME_ENGINE_SYNC = True


class T:
    __slots__ = ("name", "ap", "writers", "readers", "dsem", "dcount", "parts")

    def __init__(self, name, ap, parts=None):
        self.name = name
        self.ap = ap
        self.parts = parts
        self.writers = {}
        self.readers = {}
        self.dsem = None
        self.dcount = 0

    def __getitem__(self, k):
        return self.ap[k]


class Op:
    __slots__ = ("idx", "eng", "fn", "deps", "is_dma", "signal", "sem", "val", "dst", "tag", "iname")

    def __init__(self, idx, eng, fn, is_dma):
        self.idx = idx
        self.eng = eng
        self.fn = fn
        self.is_dma = is_dma
        self.deps = []
        self.signal = False
        self.sem = None
        self.val = 0
        self.dst = None
        self.tag = ""
        self.iname = None


class Prog:
    ENGS = ("pe", "act", "dve", "pool", "sp")

    def __init__(self, nc):
        self.nc = nc
        self.ops = []
        self._psi = 0
        self.psb = []
        self.tag = ""

    def sb(self, name, shape, dtype=F32):
        return T(name, self.nc.alloc_sbuf_tensor(name, list(shape), dtype).ap())

    def dram(self, name, shape, dtype, kind="Internal"):
        return T(name, self.nc.dram_tensor(name, list(shape), dtype, kind=kind).ap())

    def init_psum(self):
        self.psb = [T("ps%d" % i, self.nc.alloc_psum_tensor("ps%d" % i, [128, 512], F32).ap()) for i in range(8)]

    def nps(self):
        t = self.psb[self._psi % 7]
        self._psi += 1
        return t

    def op(self, eng, fn, R=(), W=(), dma=False, nowaw=False, semt=None):
        o = Op(len(self.ops), eng, fn, dma)
        o.tag = self.tag
        deps = {}
        R = [p for t in R for p in (t.parts or (t,))]
        W = [p for t in W for p in (t.parts or (t,))]
        for t in R:
            for w in t.writers.values():
                deps[w.idx] = w
        for t in W:
            if not nowaw:
                for w in t.writers.values():
                    deps[w.idx] = w
            for r in t.readers.values():
                deps[r.idx] = r
        for d in deps.values():
            if (not d.is_dma) and (not dma) and d.eng == eng:
                if eng == "pe" or not SAME_ENGINE_SYNC:
                    continue
            o.deps.append(d)
        key = ("dma", o.idx) if dma else eng
        if fn is not None:
            for t in R:
                t.readers[key] = o
            for t in W:
                if nowaw:
                    t.writers[key] = o
                else:
                    t.writers = {key: o}
                t.readers = {}
        if dma:
            assert len(W) == 1
            o.dst = semt or W[0]
        self.ops.append(o)
        return o

    def mm(self, out, lhsT, rhs, start, stop, R, W):
        self.op("pe", lambda e: e.matmul(out, lhsT=lhsT, rhs=rhs, start=start, stop=stop), R, W)

    def tr(self, out, in_, ident, R, W):
        self.op("pe", lambda e: e.transpose(out=out, in_=in_, identity=ident), R, W)

    def act(self, out, in_, func, R, W, scale=1.0, bias=0.0, accum=None):
        self.op("act", lambda e: e.activation(out=out, in_=in_, func=func, bias=bias, scale=scale, accum_out=accum), R, W)

    def tt(self, out, in0, in1, op, R, W, eng="dve"):
        self.op(eng, lambda e: e.tensor_tensor(out=out, in0=in0, in1=in1, op=op), R, W)

    def ts(self, out, in0, s1, op0, R, W, s2=None, op1=None, eng="dve"):
        if op1 is None:
            self.op(eng, lambda e: e.tensor_scalar(out=out, in0=in0, scalar1=s1, scalar2=None, op0=op0), R, W)
        else:
            self.op(eng, lambda e: e.tensor_scalar(out=out, in0=in0, scalar1=s1, scalar2=s2, op0=op0, op1=op1), R, W)

    def stt(self, out, in0, scalar, in1, op0, op1, R, W):
        self.op("dve", lambda e: e.scalar_tensor_tensor(out=out, in0=in0, scalar=scalar, in1=in1, op0=op0, op1=op1), R, W)

    def cp(self, out, in_, R, W, eng="act"):
        if eng == "act":
            self.op("act", lambda e: e.copy(out=out, in_=in_), R, W)
        else:
            self.op(eng, lambda e: e.tensor_copy(out=out, in_=in_), R, W)

    def memset(self, ap, val, W, eng="pool"):
        self.op(eng, lambda e: e.memset(ap, val), (), W)

    def dma(self, out, in_, R, W, eng="sp", nowaw=False, semt=None):
        self.op(eng, lambda e: e.dma_start(out=out, in_=in_), R, W, dma=True, nowaw=nowaw, semt=semt)

    def emit(self):
        nc = self.nc
        for o in self.ops:
            for d in o.deps:
                d.signal = True
        esem = {e: nc.alloc_semaphore("sem_" + e) for e in self.ENGS}
        cnt = {e: 0 for e in self.ENGS}
        nsem = len(esem)
        for o in self.ops:
            if o.is_dma:
                t = o.dst
                if t.dsem is None:
                    t.dsem = nc.alloc_semaphore("dsem_%d" % nsem)
                    nsem += 1
                t.dcount += 1
                o.sem = t.dsem
                o.val = 16 * t.dcount
                o.signal = True
            elif o.signal:
                cnt[o.eng] += 1
                o.sem = esem[o.eng]
                o.val = cnt[o.eng]
        self.nsem = nsem
        streams = {e: [o for o in self.ops if o.eng == e] for e in self.ENGS}

        def run(ename):
            def body(eng):
                known = {}
                for o in streams[ename]:
                    need = {}
                    for d in o.deps:
                        k = id(d.sem)
                        if d.val > known.get(k, 0) and d.val > need.get(k, (None, 0))[1]:
                            need[k] = (d.sem, d.val)
                    for k, (sm, v) in need.items():
                        eng.wait_ge(sm, v)
                        known[k] = v
                    if o.fn is None:
                        continue
                    ins = o.fn(eng)
                    try:
                        o.iname = ins.ins.name
                    except Exception:
                        pass
                    if o.signal:
                        ins.then_inc(o.sem, 16 if o.is_dma else 1)
            return body

        with nc.Block() as block:
            block.tensor(run("pe"))
            block.scalar(run("act"))
            block.vector(run("dve"))
            block.gpsimd(run("pool"))
            block.sync(run("sp"))


class Builder:
    def __init__(self, T_len, depth):
        self.Tn = T_len
        self.depth = depth
        self.NT = T_len // TT
        self.NCH = T_len // 128
        self.n_even = (depth + 1) // 2
        self.n_odd = depth // 2
        self.nc = bass.Bass("TRN2", target_bir_lowering=False)
        self.P = Prog(self.nc)
        self.build()

    def load_w(self, src_t, src_ap, a, b):
        P = self.P
        i = self._wi
        self._wi += 1
        wb = self.wbf[i % len(self.wbf)]
        n = a * b
        wv = wb[:, 0:n].rearrange("p (a b) -> p a b", a=a)
        blk = self._blk
        self._blk += 1
        cache = self.wcache[self.cur_layer]
        if self.cur_tt == 0:
            st = self.wst[self._si % len(self.wst)]
            self._si += 1
            sv = st[:, 0:n].rearrange("p (a b) -> p a b", a=a)
            P.dma(sv, src_ap, [src_t], [st])
            P.cp(wb[:, 0:n], st[:, 0:n], [st], [wb], eng="pool")
            P.dma(cache[blk, :, 0:n], wb[:, 0:n], [wb], [cache], eng="pool", nowaw=True, semt=self.wcs[i % len(self.wbf)])
        else:
            P.dma(wb[:, 0:n], cache[blk, :, 0:n], [cache], [wb])
        return wb, wv

    def wcols(self, src_t, mat_ap, c0, n):
        return self.load_w(src_t, mat_ap[:, c0:c0 + n].rearrange("(k p) n -> p k n", p=128), KC, n)

    def mm_fm(self, ps, M, wb, wv, c0, rhs_t, nk=KC, ncol=TT):
        for k in range(nk):
            self.P.mm(ps[0:M, 0:ncol], wv[:, k, c0:c0 + M], rhs_t[:, k, 0:ncol], k == 0, k == nk - 1, [wb, rhs_t], [ps])

    def mm_tm(self, ps, s, wb, wv, c0, N, lhs_t):
        for k in range(KC):
            self.P.mm(ps[:, 0:N], lhs_t[:, k, s * 128:(s + 1) * 128], wv[:, k, c0:c0 + N], k == 0, k == KC - 1, [wb, lhs_t], [ps])

    def rms_a(self, xT):
        P = self.P
        bufs = [self.mean, self.tmpB]
        for k in range(KC):
            b = bufs[k % 2]
            P.act(b[:, :], xT[:, k, :], AF.Square, [xT], [b])
            if k == 1:
                P.tt(self.rstd[:, :], bufs[0][:, :], bufs[1][:, :], ALU.add, [bufs[0], bufs[1]], [self.rstd])
            elif k > 1:
                P.tt(self.rstd[:, :], self.rstd[:, :], b[:, :], ALU.add, [self.rstd, b], [self.rstd])

    def rms_b(self, gcol0, out_t, xT):
        P = self.P
        ps = P.nps()
        P.mm(ps[:, :], self.ones_f[:, :], self.rstd[:, :], True, True, [self.rstd, self.ones_f], [ps])
        P.act(self.rstd[:, :], ps[:, :], AF.Sqrt, [ps], [self.rstd], scale=1.0 / D, bias=self.epsc[:, 0:1])
        P.op("dve", lambda e: e.reciprocal(out=self.rstd[:, :], in_=self.rstd[:, :]), [self.rstd], [self.rstd])
        for k in range(KC):
            P.stt(out_t[:, k, :], xT[:, k, :], self.ng[:, gcol0 + k:gcol0 + k + 1], self.rstd[:, :], ALU.mult, ALU.mult,
                  [xT, self.ng, self.rstd], [out_t])

    def rmsnorm(self, gcol0, out_t, xT=None):
        P = self.P
        xT = xT or self.xT
        ps = P.nps()
        sqs = [self.sq, self.sq2]
        for k in range(KC):
            sq = sqs[k % 2]
            P.act(sq[:, :], xT[:, k, :], AF.Square, [xT], [sq])
            P.mm(ps[:, :], self.ones_bf[:, :], sq[:, :], k == 0, k == KC - 1, [sq, self.ones_bf], [ps])
        P.act(self.rstd[:, :], ps[:, :], AF.Sqrt, [ps], [self.rstd], scale=1.0 / D, bias=self.epsc[:, 0:1])
        P.op("dve", lambda e: e.reciprocal(out=self.rstd[:, :], in_=self.rstd[:, :]), [self.rstd], [self.rstd])
        for k in range(KC):
            P.stt(out_t[:, k, :], xT[:, k, :], self.ng[:, gcol0 + k:gcol0 + k + 1], self.rstd[:, :], ALU.mult, ALU.mult,
                  [xT, self.ng, self.rstd], [out_t])

    def proj_add(self, src_t, mat_ap, in_t, nk):
        P = self.P
        if nk <= 8:
            for cb in range(D // 256):
                wb, wv = self.load_w(src_t, mat_ap[:, cb * 256:(cb + 1) * 256].rearrange("(k p) n -> p k n", p=128), nk, 256)
                for mi in range(2):
                    m = cb * 2 + mi
                    ps = P.nps()
                    self.mm_fm(ps, 128, wb, wv, mi * 128, in_t, nk=nk)
                    P.tt(self.xT[:, m, :], self.xT[:, m, :], ps[:, :], ALU.add, [self.xT, ps], [self.xT])
        else:
            hk = nk // 2
            for m in range(KC):
                ps = P.nps()
                for half in range(2):
                    wb, wv = self.load_w(src_t, mat_ap[half * hk * 128:(half + 1) * hk * 128, m * 128:(m + 1) * 128].rearrange("(k p) n -> p k n", p=128), hk, 128)
                    for k in range(hk):
                        kk = half * hk + k
                        P.mm(ps[:, :], wv[:, k, :], in_t[:, kk, :], kk == 0, kk == nk - 1, [wb, in_t], [ps])
                P.tt(self.xT[:, m, :], self.xT[:, m, :], ps[:, :], ALU.add, [self.xT, ps], [self.xT])

    def ffn(self, l, hooks=None):
        P = self.P
        for cb in range(0, FH, 256):
            n = 256
            wb1, wv1 = self.wcols(self.w1, self.w1[l], cb, n)
            wb3, wv3 = self.wcols(self.w3, self.w3[l], cb, n)
            for ji in range(n // 128):
                j = cb // 128 + ji
                p1 = P.nps()
                p3 = P.nps()
                self.mm_fm(p1, 128, wb1, wv1, ji * 128, self.hT)
                self.mm_fm(p3, 128, wb3, wv3, ji * 128, self.hT)
                P.act(self.tmpA[:, :], p1[:, :], AF.Silu, [p1], [self.tmpA])
                P.tt(self.actT[:, j, :], self.tmpA[:, :], p3[:, :], ALU.mult, [self.tmpA, p3], [self.actT])
                if hooks and j in hooks:
                    tg = P.tag
                    for fn_ in hooks[j]:
                        fn_()
                    P.tag = tg
        P.tag = "ffn2"
        self.proj_add(self.w2, self.w2[l], self.actT, FJ)

    def even_layer_consts(self, e):
        P = self.P
        for c8 in range(8):
            for j in range(4):
                col = ((e * 8 + c8) * 4 + j)
                P.ts(self.dgq[:, c8, j, :], self.ident_f[:, :], self.qw[:, col:col + 1], ALU.mult, [self.ident_f, self.qw], [self.dgq], eng="pool")
        P.ts(self.nbf[:, 0:1], self.gbf[:, e:e + 1], -1.0, ALU.mult, [self.gbf], [self.nbf])
        for h in range(4):
            P.memset(self.CH[h][:, :], 0.0, [self.CH[h]])
            P.memset(self.CB[h][:, :], 0.0, [self.CB[h]])
        P.memset(self.DL[:, :], 0.0, [self.DL])
        P.memset(self.Blast[:, :], 0.0, [self.Blast])
        P.memset(self.Mlast[:, :], 0.0, [self.Mlast])
        P.memset(self.E2L[:, :, :], 1.0, [self.E2L])

    def even_mixer(self, e, tt):
        P = self.P
        win = self.ev_w_in
        W = self.ev_w_in[e]
        first = (tt == 0)
        P.tag = "ev_proj"
        for cc in range(4):
            if cc % 2 == 0:
                wba, wva = self.wcols(win, W, cc * 128, 256)
                wbg, wvg = self.wcols(win, W, 512 + cc * 128, 256)
            if first:
                P.memset(self.UB[cc][:, 0:30], 0.0, [self.UB[cc]])
            else:
                P.cp(self.UB[cc][:, 0:30], self.UB[cc][:, TT:TT + 30], [self.UB[cc]], [self.UB[cc]], eng="pool")
            pa = P.nps()
            pg = P.nps()
            self.mm_fm(pa, 128, wba, wva, (cc % 2) * 128, self.hT)
            self.mm_fm(pg, 128, wbg, wvg, (cc % 2) * 128, self.hT)
            P.act(self.tmpA[:, :], pg[:, :], AF.Sigmoid, [pg], [self.tmpA])
            P.tt(self.UB[cc][:, 30:30 + TT], pa[:, :], self.tmpA[:, :], ALU.mult, [pa, self.tmpA], [self.UB[cc]])
        for c8 in range(8):
                if c8 % 2 == 0:
                    wb, wv = self.wcols(win, W, 1024 + c8 * 128, 256)
                if first:
                    P.memset(self.QKB[c8][:, 0:3], 0.0, [self.QKB[c8]])
                else:
                    P.cp(self.QKB[c8][:, 0:3], self.QKB[c8][:, TT:TT + 3], [self.QKB[c8]], [self.QKB[c8]], eng="pool")
                ps = P.nps()
                self.mm_fm(ps, 128, wb, wv, (c8 % 2) * 128, self.hT)
                P.cp(self.QKB[c8][:, 3:3 + TT], ps[:, :], [ps], [self.QKB[c8]])
        for half in range(2):
            wb, wv = self.wcols(win, W, 2048 + half * 256, 256)
            for s in range(NS):
                ps = P.nps()
                self.mm_tm(ps, s, wb, wv, 0, 256, self.hT)
                P.cp(self.VA[s][:, half * 2:half * 2 + 2, 0:128], ps[:, 0:256].rearrange("p (h e) -> p h e", h=2), [ps], [self.VA[s]])
        for half in range(2):
            wb, wv = self.wcols(win, W, 2560 + half * 256, 256)
            for s in range(NS):
                ps = P.nps()
                self.mm_tm(ps, s, wb, wv, 0, 256, self.hT)
                P.act(self.SG[s][:, half * 256:(half + 1) * 256], ps[:, 0:256], AF.Sigmoid, [ps], [self.SG[s]])
        P.tag = "ev_qkconv"
        for c8 in range(8):
            ps = P.nps()
            for j in range(4):
                P.mm(ps[:, :], self.dgq[:, c8, j, :], self.QKB[c8][:, j:j + TT], j == 0, j == 3, [self.dgq, self.QKB[c8]], [ps])
            col = e * 8 + c8
            P.act(self.QK[c8][:, :], ps[:, :], AF.Silu, [ps, self.qb], [self.QK[c8]], bias=self.qb[:, col:col + 1])
        P.tag = "ev_gates"
        wb, wv = self.wcols(win, W, 3072, 8)
        pgi = P.nps()
        pgf = P.nps()
        self.mm_fm(pgi, 4, wb, wv, 0, self.hT)
        self.mm_fm(pgf, 4, wb, wv, 4, self.hT)
        g = self.G
        P.act(g["ig"][:, :], pgi[0:4, :], AF.Identity, [pgi, self.gbi], [g["ig"]], bias=self.gbi[:, e:e + 1])
        P.act(g["sp"][:, :], pgf[0:4, :], AF.Exp, [pgf, self.nbf], [g["sp"]], scale=-1.0, bias=self.nbf[:, 0:1])
        P.act(g["sp"][:, :], g["sp"][:, :], AF.Ln, [g["sp"]], [g["sp"]], bias=1.0)
        P.op("dve", lambda en: en.tensor_tensor_scan(out=g["B"][:, :], data0=self.RM[0:4, :], data1=g["sp"][:, :], initial=0.0, op0=ALU.mult, op1=ALU.add),
             [self.RM, g["sp"]], [g["B"]])
        P.tt(g["a"][:, :], g["ig"][:, :], g["B"][:, :], ALU.add, [g["ig"], g["B"]], [g["a"]])
        P.memset(self.DL[:, :], 0.0, [self.DL])
        Bv = g["B"][:, :].rearrange("p (c l) -> p c l", l=128)
        DLv = self.DL[:, :].rearrange("p (c l) -> p c l", l=128)
        P.ts(DLv[:, 1:NS, 0:1], Bv[:, 0:NS - 1, 127:128], -1.0, ALU.mult, [g["B"]], [self.DL])
        P.ts(self.DL[:, 0:1], self.Blast[:, 0:1], -1.0, ALU.mult, [self.Blast], [self.DL])
        P.op("dve", lambda en: en.tensor_tensor_scan(out=g["M"][:, :], data0=self.DL[:, :], data1=g["a"][:, :], initial=self.Mlast[:, 0:1], op0=ALU.add, op1=ALU.max),
             [self.DL, g["a"], self.Mlast], [g["M"]])
        Mv = g["M"][:, :].rearrange("p (c l) -> p c l", l=128)
        P.cp(g["msh"][:, 1:NS], Mv[:, 0:NS - 1, 127], [g["M"]], [g["msh"]], eng="dve")
        P.cp(g["msh"][:, 0:1], self.Mlast[:, 0:1], [self.Mlast], [g["msh"]], eng="dve")
        P.tt(g["mp"][:, :], g["msh"][:, :], DLv[:, :, 0], ALU.add, [g["msh"], self.DL], [g["mp"]])
        P.ts(g["nmp"][:, :], g["mp"][:, :], -1.0, ALU.mult, [g["mp"]], [g["nmp"]], s2=math.log(128.0 ** -0.5), op1=ALU.add)
        P.tt(g["bm"][:, :], g["B"][:, :], g["M"][:, :], ALU.subtract, [g["B"], g["M"]], [g["bm"]])
        for c in range(NS):
            sl = slice(c * 128, (c + 1) * 128)
            P.act(g["E1"][:, sl], g["a"][:, sl], AF.Exp, [g["a"], g["nmp"]], [g["E1"]], bias=g["nmp"][:, c:c + 1])
            P.act(g["E2"][:, sl], g["M"][:, sl], AF.Exp, [g["M"], g["mp"]], [g["E2"]], scale=-1.0, bias=g["mp"][:, c:c + 1])
        P.act(g["E3"][:, :], g["bm"][:, :], AF.Exp, [g["bm"]], [g["E3"]])
        P.cp(self.Blast[:, 0:1], g["B"][:, TT - 1:TT], [g["B"]], [self.Blast], eng="dve")
        P.cp(self.Mlast[:, 0:1], g["M"][:, TT - 1:TT], [g["M"]], [self.Mlast], eng="dve")
        P.tag = "ev_conv"
        ps1 = P.nps()
        ps2 = P.nps()
        for cc in range(4):
            ps = P.nps()
            base = (e * 4 + cc) * 31
            for dgt, j0, nj in ((self.dgA, 0, 16), (self.dgB, 16, 15)):
                P.tt(dgt[:, 0:nj, :], self.ident_f[:, :].unsqueeze(1).to_broadcast([128, nj, 128]),
                     self.cw[:, base + j0:base + j0 + nj].unsqueeze(2).to_broadcast([128, nj, 128]), ALU.mult, [self.ident_f, self.cw], [dgt], eng="pool")
            for j in range(31):
                dgt, jj = (self.dgA, j) if j < 16 else (self.dgB, j - 16)
                P.mm(ps[:, :], dgt[:, jj, :], self.UB[cc][:, j:j + TT], j == 0, j == 30, [dgt, self.UB[cc]], [ps])
            col = e * 4 + cc
            P.act(self.Y[cc][:, :], ps[:, :], AF.Identity, [ps, self.cb], [self.Y[cc]], bias=self.cb[:, col:col + 1])
            P.act(self.tmpA[:, :], self.Y[cc][:, :], AF.Square, [self.Y[cc]], [self.tmpA])
            P.mm(ps1[:, :], self.ones_f[:, :], self.Y[cc][:, :], cc == 0, cc == 3, [self.ones_f, self.Y[cc]], [ps1])
            P.mm(ps2[:, :], self.ones_f[:, :], self.tmpA[:, :], cc == 0, cc == 3, [self.ones_f, self.tmpA], [ps2])
        P.ts(self.mean[:, :], ps1[:, :], 1.0 / 512, ALU.mult, [ps1], [self.mean])
        P.tt(self.tmpB[:, :], self.mean[:, :], self.mean[:, :], ALU.mult, [self.mean], [self.tmpB])
        P.stt(self.rstd[:, :], ps2[:, :], 1.0 / 512, self.tmpB[:, :], ALU.mult, ALU.subtract, [ps2, self.tmpB], [self.rstd])
        P.act(self.rstd[:, :], self.rstd[:, :], AF.Sqrt, [self.rstd], [self.rstd], bias=self.epsc[:, 0:1])
        P.op("dve", lambda en: en.reciprocal(out=self.rstd[:, :], in_=self.rstd[:, :]), [self.rstd], [self.rstd])
        for cc in range(4):
            col = e * 4 + cc
            P.tt(self.tmpB[:, :], self.Y[cc][:, :], self.mean[:, :], ALU.subtract, [self.Y[cc], self.mean], [self.tmpB])
            P.tt(self.tmpB[:, :], self.tmpB[:, :], self.rstd[:, :], ALU.mult, [self.tmpB, self.rstd], [self.tmpB])
            P.act(self.mixT[:, cc, :], self.tmpB[:, :], AF.Silu, [self.tmpB, self.lng, self.lnb], [self.mixT],
                  scale=self.lng[:, col:col + 1], bias=self.lnb[:, col:col + 1])
        P.tag = "ev_gates2"
        pst = P.nps()
        for s in range(NS):
            sl = slice(s * 128, (s + 1) * 128)
            for qi, nm in enumerate(("E1", "E2", "E3")):
                P.mm(pst[:, s * 12 + qi * 4: s * 12 + qi * 4 + 4], g[nm][0:4, sl], self.ident_f[0:4, 0:4], True, True, [g[nm], self.ident_f], [pst])
        P.cp(self.ETM[:, :, :], pst[:, 0:NS * 12].rearrange("p (s q) -> p s q", s=NS), [pst], [self.ETM], eng="dve")
        E2v = g["E2"][:, :].rearrange("p (c l) -> p c l", l=128)
        P.tt(g["e2m"][:, :, :], E2v[:, :, 127].unsqueeze(1).to_broadcast([4, 4, NS]), self.BM[:, :, :], ALU.mult, [g["E2"], self.BM], [g["e2m"]])
        psb = P.nps()
        P.mm(psb[:, 0:4 * NS], self.ones_f[0:4, :], g["e2m"][:, :, :].rearrange("p h c -> p (h c)"), True, True, [g["e2m"], self.ones_f], [psb])
        P.cp(self.E2L[:, :, 1 + tt * NS:1 + (tt + 1) * NS], psb[:, 0:4 * NS].rearrange("p (h c) -> p h c", h=4), [psb], [self.E2L], eng="dve")
        P.tag = "mlstm"
        pb = P.psb

        def st1(s):
            sl = slice(s * 128, (s + 1) * 128)
            pS = pb[s % 2]
            for h in range(4):
                P.mm(pS[:, h * 128:(h + 1) * 128], self.QK[4 + h][:, sl], self.QK[h][:, sl], True, True, [self.QK[4 + h], self.QK[h]], [pS])
            pK = pb[2]
            pKb = pK[:, :].bitcast(BF16)
            for h in range(4):
                P.tr(pKb[:, h * 128:(h + 1) * 128], self.QK[4 + h][:, sl], self.ident_b[:, :], [self.QK[4 + h], self.ident_b], [pK])
            for h in range(4):
                e1 = self.ETM[:, s, h:h + 1]
                P.stt(self.PT[h][:, :], pS[:, h * 128:(h + 1) * 128], e1, self.CMASK[:, :], ALU.mult, ALU.mult, [pS, self.ETM, self.CMASK], [self.PT[h]])
                P.act(self.KH[h][:, :], pKb[:, h * 128:(h + 1) * 128], AF.Identity, [pK, self.ETM], [self.KH[h]], scale=e1)

        def st2(s):
            gch = tt * NS + s
            sl = slice(s * 128, (s + 1) * 128)
            pO = [pb[4], pb[5]]
            for h in range(4):
                po = pO[h // 2][:, (h % 2) * 129:(h % 2) * 129 + 129]
                va = self.VA[s][:, h, 0:129]
                P.mm(po, self.PT[h][:, :], va, True, False, [self.PT[h], self.VA[s]], [pO[h // 2]])
                P.mm(po, self.QK[h][:, sl], self.CB[h][:, 0:129], False, True, [self.QK[h], self.CB[h]], [pO[h // 2]])
                pU = pb[6 + h % 2]
                P.mm(pU[:, 0:129], self.KH[h][:, :], va, True, True, [self.KH[h], self.VA[s]], [pU])
                P.stt(self.CH[h][:, :], self.CH[h][:, :], self.E2L[:, h, gch:gch + 1], pU[:, 0:129], ALU.mult, ALU.add, [self.CH[h], self.E2L, pU], [self.CH[h]])
                P.act(self.CB[h][:, 0:129], self.CH[h][:, :], AF.Identity, [self.CH[h], self.E2L], [self.CB[h]], scale=self.E2L[:, h, gch + 1:gch + 2])
            return pO

        def st3(s, pO):
            sl = slice(s * 128, (s + 1) * 128)
            for hp in range(2):
                den = pO[hp][:, 0:258].rearrange("p (h e) -> p h e", h=2)[:, :, 128]
                P.tt(self.R[:, hp * 2:hp * 2 + 2], den, self.ETM[:, s, 4 + hp * 2:6 + hp * 2], ALU.mult, [pO[hp], self.ETM], [self.R])
            P.stt(self.R2[:, :], self.R[:, :], -1.0, self.R[:, :], ALU.mult, ALU.max, [self.R], [self.R2])
            P.tt(self.R[:, :], self.R2[:, :], self.ETM[:, s, 8:12], ALU.max, [self.R2, self.ETM], [self.R])
            P.op("dve", lambda en: en.reciprocal(out=self.R[:, :], in_=self.R[:, :]), [self.R], [self.R])
            P.tt(self.R[:, :], self.R[:, :], self.ETM[:, s, 4:8], ALU.mult, [self.R, self.ETM], [self.R])
            for h in range(4):
                po = pO[h // 2][:, (h % 2) * 129:(h % 2) * 129 + 128]
                P.stt(self.HH[h][:, :], po, self.R[:, h:h + 1], self.SG[s][:, h * 128:(h + 1) * 128], ALU.mult, ALU.mult, [pO[h // 2], self.R, self.SG[s]], [self.HH[h]])
                P.act(self.junk[:, 0:128], self.HH[h][:, :], AF.Square, [self.HH[h]], [self.junk, self.SS], accum=self.SS[:, h:h + 1])
            P.act(self.RS[:, :], self.SS[:, :], AF.Sqrt, [self.SS], [self.RS], scale=1.0 / 128, bias=self.epsc[:, 0:1])
            P.op("dve", lambda en: en.reciprocal(out=self.RS[:, :], in_=self.RS[:, :]), [self.RS], [self.RS])
            pT = pb[3]
            pTb = pT[:, :].bitcast(BF16)
            for h in range(4):
                P.stt(self.HM[:, h * 128:(h + 1) * 128], self.HH[h][:, :], self.RS[:, h:h + 1], self.ehg[:, h * 128:(h + 1) * 128], ALU.mult, ALU.mult,
                      [self.HH[h], self.RS, self.ehg], [self.HM])
                P.tr(pTb[:, h * 128:(h + 1) * 128], self.HM[:, h * 128:(h + 1) * 128], self.ident_b[:, :], [self.HM, self.ident_b], [pT])
            P.cp(self.mixT[:, 4:8, sl], pTb[:, 0:512].rearrange("p (h t) -> p h t", h=4), [pT], [self.mixT])

        st1(0)
        for s in range(NS):
            pO = st2(s)
            if s + 1 < NS:
                st1(s + 1)
            st3(s, pO)

    def odd_layer_consts(self, o):
        P = self.P
        for h in range(4):
            P.memset(self.SH[h][:, :], 0.0, [self.SH[h]])
            P.memset(self.SB[h][:, :], 0.0, [self.SB[h]])
            P.memset(self.EA[h][:, :], 1.0, [self.EA[h]])
        P.ts(self.ngb[:, :], self.ogb[:, o * 4:o * 4 + 4], -1.0, ALU.mult, [self.ogb], [self.ngb])
        P.dma(self.w2g_st[:, :], self.od_gate_w2[o], [self.od_gate_w2], [self.w2g_st])
        P.cp(self.w2g[:, :], self.w2g_st[:, :], [self.w2g_st], [self.w2g], eng="pool")

    def odd_mixer(self, o, tt):
        P = self.P
        win = self.od_w_in
        W = self.od_w_in[o]
        P.tag = "od_proj"
        wb, wv = self.wcols(win, W, 3072, 16)
        pg = P.nps()
        self.mm_fm(pg, 16, wb, wv, 0, self.hT)
        P.cp(self.GLB[:, :], pg[0:16, :], [pg], [self.GLB])
        for h in range(4):
            if h % 2 == 0:
                wbq, wvq = self.wcols(win, W, h * 128, 256)
                wbk, wvk = self.wcols(win, W, 512 + h * 128, 256)
            pl = P.nps()
            P.mm(pl[:, :], self.w2g[0:16, h * 128:(h + 1) * 128], self.GLB[0:16, :], True, True, [self.w2g, self.GLB], [pl])
            P.act(self.tmpA[:, :], pl[:, :], AF.Exp, [pl, self.ngb], [self.tmpA], scale=-1.0, bias=self.ngb[:, h:h + 1])
            P.act(self.tmpA[:, :], self.tmpA[:, :], AF.Ln, [self.tmpA], [self.tmpA], bias=1.0)
            P.op("dve", lambda en: en.tensor_tensor_scan(out=self.tmpB[:, :], data0=self.RM[:, :], data1=self.tmpA[:, :], initial=0.0, op0=ALU.mult, op1=ALU.add),
                 [self.RM, self.tmpA], [self.tmpB])
            P.act(self.EBQ[:, :], self.tmpB[:, :], AF.Exp, [self.tmpB], [self.EBQ], scale=-1.0 / 16)
            P.act(self.EBK[:, :], self.tmpB[:, :], AF.Exp, [self.tmpB], [self.EBK], scale=1.0 / 16)
            pq = P.nps()
            pk = P.nps()
            self.mm_fm(pq, 128, wbq, wvq, (h % 2) * 128, self.hT)
            self.mm_fm(pk, 128, wbk, wvk, (h % 2) * 128, self.hT)
            P.stt(self.QK[h][:, :], pq[:, :], 128.0 ** -0.5, self.EBQ[:, :], ALU.mult, ALU.mult, [pq, self.EBQ], [self.QK[h]])
            P.tt(self.QK[4 + h][:, :], pk[:, :], self.EBK[:, :], ALU.mult, [pk, self.EBK], [self.QK[4 + h]])
            EBv = self.EBQ[:, :].rearrange("p (c l) -> p c l", l=128)
            P.cp(self.EA[h][:, 1 + tt * NS:1 + (tt + 1) * NS], EBv[:, :, 127], [self.EBQ], [self.EA[h]], eng="dve")
        for q4 in range(4):
            wb, wv = self.wcols(win, W, 1024 + q4 * 256, 256)
            for s in range(NS):
                ps = P.nps()
                self.mm_tm(ps, s, wb, wv, 0, 256, self.hT)
                P.cp(self.VG[s][:, q4 * 256:(q4 + 1) * 256], ps[:, 0:256], [ps], [self.VG[s]])
        for q4 in range(4):
            wb, wv = self.wcols(win, W, 2048 + q4 * 256, 256)
            for s in range(NS):
                ps = P.nps()
                self.mm_tm(ps, s, wb, wv, 0, 256, self.hT)
                P.act(self.tmpA[:, 0:256], ps[:, 0:256], AF.Silu, [ps], [self.tmpA])
                P.tt(self.GR[s][:, q4 * 256:(q4 + 1) * 256], self.tmpA[:, 0:256], self.ohg[:, q4 * 256:(q4 + 1) * 256], ALU.mult, [self.tmpA, self.ohg], [self.GR[s]], eng="pool")
        P.tag = "gla"
        pb = P.psb

        def st1(s):
            sl = slice(s * 128, (s + 1) * 128)
            pA = pb[s % 2]
            for h in range(4):
                P.mm(pA[:, h * 128:(h + 1) * 128], self.QK[4 + h][:, sl], self.QK[h][:, sl], True, True, [self.QK[4 + h], self.QK[h]], [pA])
            pK = pb[2]
            pKb = pK[:, :].bitcast(BF16)
            for h in range(4):
                P.tr(pKb[:, h * 128:(h + 1) * 128], self.QK[4 + h][:, sl], self.ident_b[:, :], [self.QK[4 + h], self.ident_b], [pK])
            P.tt(self.AT[:, :], pA[:, :], self.CMASK4[:, :], ALU.mult, [pA, self.CMASK4], [self.AT])
            P.cp(self.KTM[:, :], pKb[:, 0:512], [pK], [self.KTM])

        def st2(s):
            gch = tt * NS + s
            sl = slice(s * 128, (s + 1) * 128)
            pO = [pb[4], pb[5]]
            for h in range(4):
                po = pO[h // 2][:, (h % 2) * 256:(h % 2) * 256 + 256]
                vh = self.VG[s][:, h * 256:(h + 1) * 256]
                P.mm(po, self.AT[:, h * 128:(h + 1) * 128], vh, True, False, [self.AT, self.VG[s]], [pO[h // 2]])
                P.mm(po, self.QK[h][:, sl], self.SB[h][:, :], False, True, [self.QK[h], self.SB[h]], [pO[h // 2]])
                pU = pb[6 + h % 2]
                P.mm(pU[:, 0:256], self.KTM[:, h * 128:(h + 1) * 128], vh, True, True, [self.KTM, self.VG[s]], [pU])
                P.stt(self.SH[h][:, :], self.SH[h][:, :], self.EA[h][:, gch:gch + 1], pU[:, 0:256], ALU.mult, ALU.add, [self.SH[h], self.EA[h], pU], [self.SH[h]])
                P.act(self.SB[h][:, :], self.SH[h][:, :], AF.Identity, [self.SH[h], self.EA[h]], [self.SB[h]], scale=self.EA[h][:, gch + 1:gch + 2])
                P.act(self.junk[:, 0:256], po, AF.Square, [pO[h // 2]], [self.junk, self.SS], accum=self.SS[:, h:h + 1])
            return pO

        def st3(s, pO):
            sl = slice(s * 128, (s + 1) * 128)
            P.act(self.RS[:, :], self.SS[:, :], AF.Sqrt, [self.SS], [self.RS], scale=1.0 / 256, bias=self.epsc[:, 0:1])
            P.op("dve", lambda en: en.reciprocal(out=self.RS[:, :], in_=self.RS[:, :]), [self.RS], [self.RS])
            pT = pb[3]
            pTb = pT[:, :].bitcast(BF16)
            for h in range(4):
                po = pO[h // 2][:, (h % 2) * 256:(h % 2) * 256 + 256]
                P.stt(self.OM[:, h * 256:(h + 1) * 256], po, self.RS[:, h:h + 1], self.GR[s][:, h * 256:(h + 1) * 256], ALU.mult, ALU.mult,
                      [pO[h // 2], self.RS, self.GR[s]], [self.OM])
                for q in range(2):
                    c = h * 2 + q
                    P.tr(pTb[:, c * 128:(c + 1) * 128], self.OM[:, c * 128:(c + 1) * 128], self.ident_b[:, :], [self.OM, self.ident_b], [pT])
            P.cp(self.mixT[:, 0:8, sl], pTb[:, 0:1024].rearrange("p (h t) -> p h t", h=8), [pT], [self.mixT])

        st1(0)
        for s in range(NS):
            pO = st2(s)
            if s + 1 < NS:
                st1(s + 1)
            st3(s, pO)

    def build(self):
        nc, P = self.nc, self.P
        Tn, depth = self.Tn, self.depth
        ne, no = self.n_even, max(self.n_odd, 1)
        ext = lambda name, shape: P.dram(name, shape, F32, kind="ExternalInput")
        self.x = ext("x", [Tn, D])
        self.ngd = ext("ng", [128, (2 * depth + 1) * KC])
        self.w1 = ext("ffn_w1", [depth, D, FH])
        self.w3 = ext("ffn_w3", [depth, D, FH])
        self.w2 = ext("ffn_w2", [depth, FH, D])
        self.ev_w_in = ext("ev_w_in", [ne, D, 3080])
        self.ev_w_out = ext("ev_w_out", [ne, D, D])
        self.od_w_in = ext("od_w_in", [no, D, 3088])
        self.od_w_out = ext("od_w_out", [no, D, D])
        self.od_gate_w2 = ext("od_gate_w2", [no, 16, 512])
        smalls = {"cw": ne * 4 * 31, "cb": ne * 4, "lng": ne * 4, "lnb": ne * 4, "qw": ne * 8 * 4, "qb": ne * 8, "ogb": no * 4}
        sm_d = {k: ext("s_" + k, [128, n]) for k, n in smalls.items()}
        gbi_d = ext("s_gbi", [4, ne])
        gbf_d = ext("s_gbf", [4, ne])
        ehg_d = ext("s_ehg", [ne, 128, 512])
        ohg_d = ext("s_ohg", [no, 128, 1024])
        self.out = P.dram("out", [Tn, D], F32, kind="ExternalOutput")
        self.xres = P.dram("xres", [D, Tn], F32)
        P.init_psum()
        self.xTs = [P.sb("xT%d" % i, [128, KC, TT]) for i in range(2)]
        self.xT = self.xTs[0]
        self.hA = P.sb("hA", [128, KC, TT], BF16)
        self.hF = P.sb("hF", [128, KC, TT], BF16)
        self.hT = self.hA
        self.mixT = P.sb("mixT", [128, KC, TT], BF16)
        self.actT = P.sb("actT", [128, FJ, TT], BF16)
        actf = self.actT.ap.rearrange("p j t -> p (j t)").bitcast(F32)
        self.yT = T("yT", actf[:, 0:KC * TT].rearrange("p (k t) -> p k t", k=KC), parts=[self.actT])
        self.xin = T("xin", self.mixT.ap.rearrange("p k t -> p (k t)").bitcast(F32)[:, 0:D], parts=[self.mixT])
        self.yout = T("yout", self.hF.ap.rearrange("p k t -> p (k t)").bitcast(F32)[:, 0:D], parts=[self.hF])
        self.wst = [P.sb("wst%d" % i, [128, 2048]) for i in range(2)]
        self.wbf = [P.sb("wbf%d" % i, [128, 2048], BF16) for i in range(6)]
        self._wi = 0
        self._si = 0
        self.wcs = [T("wcs%d" % i, None) for i in range(6)]
        self.wcache = [P.dram("wcache%d" % l, [80, 128, 2048], BF16) for l in range(depth)]
        self.sq = P.sb("sq", [128, TT], BF16)
        self.sq2 = P.sb("sq2", [128, TT], BF16)
        self.rstd = P.sb("rstd", [128, TT])
        self.mean = P.sb("mean", [128, TT])
        self.tmpA = P.sb("tmpA", [128, TT])
        self.tmpB = P.sb("tmpB", [128, TT])
        self.junk = P.sb("junk", [128, 256], BF16)
        self.ng = P.sb("ngs", [128, (2 * depth + 1) * KC])
        self.ident_f = P.sb("ident_f", [128, 128])
        self.ident_b = P.sb("ident_b", [128, 128], BF16)
        self.ones_f = P.sb("ones_f", [128, 128])
        self.ones_bf = P.sb("ones_bf", [128, 128], BF16)
        self.CMASK = P.sb("cmask", [128, 128])
        self.CMASK4 = P.sb("cmask4", [128, 512])
        self.RM = P.sb("rm", [128, TT])
        self.BM = P.sb("bm", [4, 4, NS])
        self.epsc = P.sb("epsc", [128, 1])
        self.fence = {e: P.sb("fence_" + e, [128, 1]) for e in ("act", "dve", "pool")}
        for k, n in smalls.items():
            setattr(self, k, P.sb("sb_" + k, [128, n]))
        self.gbi = P.sb("gbi", [4, ne])
        self.gbf = P.sb("gbf", [4, ne])
        self.nbf = P.sb("nbf", [4, 1])
        self.ehg = P.sb("ehg", [128, 512])
        self.ohg = P.sb("ohg", [128, 1024])
        self.QK = [P.sb("QK%d" % i, [128, TT], BF16) for i in range(8)]
        self.SS = P.sb("SS", [128, 4])
        self.RS = P.sb("RS", [128, 4])
        UNI_BYTES = 56 * 1024
        uni = nc.alloc_sbuf_tensor("uni", [128, UNI_BYTES // 2], BF16).ap()
        self._uoff = 0

        def carve(name, shape, dtype=F32):
            esz = 4 if dtype == F32 else 2
            n = 1
            for d_ in shape[1:]:
                n *= d_
            nb = (n * esz + 31) // 32 * 32
            assert self._uoff + nb <= UNI_BYTES, (name, self._uoff, nb)
            ap = uni[0:shape[0], self._uoff // 2:self._uoff // 2 + n * esz // 2]
            self._uoff += nb
            if dtype == F32:
                ap = ap.bitcast(F32)
            if len(shape) == 3:
                ap = ap.rearrange("p (a b) -> p a b", a=shape[1])
            elif len(shape) == 4:
                ap = ap.rearrange("p (a b c) -> p a b c", a=shape[1], b=shape[2])
            return T(name, ap)

        if ne:
            self._uoff = 0
            self.dgA = carve("dgA", [128, 16, 128], BF16)
            self.dgB = carve("dgB", [128, 15, 128], BF16)
            self.dgq = carve("dgq", [128, 8, 4, 128], BF16)
            self.UB = [carve("UB%d" % i, [128, TT + 30], BF16) for i in range(4)]
            self.QKB = [carve("QKB%d" % i, [128, TT + 4], BF16) for i in range(8)]
            self.VA = [carve("VA%d" % i, [128, 4, 130], BF16) for i in range(NS)]
            self.SG = [carve("SG%d" % i, [128, 512], BF16) for i in range(NS)]
            self.Y = [carve("Y%d" % i, [128, TT]) for i in range(4)]
            gn = ("ig", "sp", "B", "a", "M", "bm", "E1", "E2", "E3", "DL")
            self.G = {n: T("g_" + n, actf[0:4, i * TT:(i + 1) * TT], parts=[self.actT]) for i, n in enumerate(gn)}
            self.DL = self.G["DL"]
            for n in ("msh", "mp", "nmp"):
                self.G[n] = carve("g_" + n, [4, NS])
            self.G["e2m"] = carve("g_e2m", [4, 4, NS])
            self.Blast = carve("Blast", [4, 1])
            self.Mlast = carve("Mlast", [4, 1])
            self.ETM = carve("ETM", [128, NS, 12])
            self.E2L = carve("E2L", [128, 4, self.NCH + 1])
            self.PT = [carve("PT%d" % i, [128, 128], BF16) for i in range(4)]
            self.KH = [carve("KH%d" % i, [128, 128], BF16) for i in range(4)]
            self.CH = [carve("CH%d" % i, [128, 129]) for i in range(4)]
            self.CB = [carve("CB%d" % i, [128, 130], BF16) for i in range(4)]
            self.R = carve("R", [128, 4])
            self.R2 = carve("R2", [128, 4])
            self.HH = [carve("HH%d" % i, [128, 128]) for i in range(4)]
            self.HM = carve("HM", [128, 512], BF16)
        if self.n_odd:
            self._uoff = 0
            self.ngb = carve("ngb", [128, 4])
            self.w2g_st = carve("w2g_st", [16, 512])
            self.w2g = carve("w2g", [16, 512], BF16)
            self.GLB = carve("GLB", [16, TT], BF16)
            self.EBQ = carve("EBQ", [128, TT])
            self.EBK = carve("EBK", [128, TT])
            self.EA = [carve("EA%d" % i, [128, self.NCH + 1]) for i in range(4)]
            self.VG = [carve("VG%d" % i, [128, 1024], BF16) for i in range(NS)]
            self.GR = [carve("GR%d" % i, [128, 1024], BF16) for i in range(NS)]
            self.AT = carve("AT", [128, 512], BF16)
            self.KTM = carve("KTM", [128, 512], BF16)
            self.SH = [carve("SH%d" % i, [128, 256]) for i in range(4)]
            self.SB = [carve("SB%d" % i, [128, 256], BF16) for i in range(4)]
            self.OM = carve("OM", [128, 1024], BF16)
        P.memset(self.ones_f[:, :], 1.0, [self.ones_f])
        P.memset(self.ones_bf[:, :], 1.0, [self.ones_bf])
        P.memset(self.epsc[:, :], EPS, [self.epsc])
        P.memset(self.RM[:, :], 1.0, [self.RM])
        P.memset(self.RM[:, :].rearrange("p (c l) -> p c l", l=128)[:, :, 0:1], 0.0, [self.RM])
        P.op("pool", lambda e: e.affine_select(out=self.ident_f[:, :], in_=self.ones_f[:, :], pattern=[[1, 128]], compare_op=ALU.is_equal, fill=0.0, base=0, channel_multiplier=-1),
             [self.ones_f], [self.ident_f])
        P.cp(self.ident_b[:, :], self.ident_f[:, :], [self.ident_f], [self.ident_b], eng="pool")
        P.op("pool", lambda e: e.affine_select(out=self.CMASK[:, :], in_=self.ones_f[:, :], pattern=[[1, 128]], compare_op=ALU.is_ge, fill=0.0, base=0, channel_multiplier=-1),
             [self.ones_f], [self.CMASK])
        for h in range(4):
            P.cp(self.CMASK4[:, h * 128:(h + 1) * 128], self.CMASK[:, :], [self.CMASK], [self.CMASK4], eng="pool")
        P.cp(self.BM[:, :, :], self.ident_f[0:4, 0:4].unsqueeze(2).to_broadcast([4, 4, NS]), [self.ident_f], [self.BM], eng="pool")
        P.dma(self.ng[:, :], self.ngd[:, :], [self.ngd], [self.ng])
        for k in smalls:
            P.dma(getattr(self, k)[:, :], sm_d[k][:, :], [sm_d[k]], [getattr(self, k)])
        P.dma(self.gbi[:, :], gbi_d[:, :], [gbi_d], [self.gbi])
        P.dma(self.gbf[:, :], gbf_d[:, :], [gbf_d], [self.gbf])
        seq = [(l, tt) for l in range(depth) for tt in range(self.NT)]

        def load_resid(i):
            l, tt = seq[i]
            xT = self.xTs[i % 2]
            t0 = tt * TT
            tg = P.tag
            P.tag = "resid_io"
            if l == 0:
                for s in range(NS):
                    P.dma(self.xin[:, :], self.x[t0 + s * 128:t0 + (s + 1) * 128, :], [self.x], [self.xin])
                    for half in range(2):
                        ps = P.nps()
                        for q in range(4):
                            k = half * 4 + q
                            P.tr(ps[:, q * 128:(q + 1) * 128], self.xin[:, k * 128:(k + 1) * 128], self.ident_f[:, :], [self.xin, self.ident_f], [ps])
                        P.cp(xT[:, half * 4:half * 4 + 4, s * 128:(s + 1) * 128], ps[:, :].rearrange("p (k t) -> p k t", k=4), [ps], [xT], eng="dve")
            else:
                P.dma(xT[:, :, :], self.xres[:, t0:t0 + TT].rearrange("(k p) t -> p k t", p=128), [self.xres], [xT])
            P.tag = tg

        def norm1(i):
            l, tt = seq[i]
            P.tag = "norm1"
            self.rmsnorm(l * 2 * KC, self.hA, self.xTs[i % 2])

        def norm1_hooks(i):
            l, tt = seq[i]
            xT = self.xTs[i % 2]
            g0 = l * 2 * KC
            pbn = P.psb[7]
            sqs = [self.sq, self.sq2]
            hk = {}

            def add(j, f):
                hk.setdefault(j, []).append(f)

            def mk_sq(k):
                def f():
                    P.tag = "norm1"
                    P.act(sqs[k % 2][:, :], xT[:, k, :], AF.Square, [xT], [sqs[k % 2]])
                return f

            def mk_mm(k):
                def f():
                    P.tag = "norm1"
                    P.mm(pbn[:, :], self.ones_bf[:, :], sqs[k % 2][:, :], k == 0, k == KC - 1, [sqs[k % 2], self.ones_bf], [pbn])
                return f

            def fin():
                P.tag = "norm1"
                P.act(self.rstd[:, :], pbn[:, :], AF.Sqrt, [pbn], [self.rstd], scale=1.0 / D, bias=self.epsc[:, 0:1])
                P.op("dve", lambda e: e.reciprocal(out=self.rstd[:, :], in_=self.rstd[:, :]), [self.rstd], [self.rstd])

            def mk_stt(k):
                def f():
                    P.tag = "norm1"
                    P.stt(self.hA[:, k, :], xT[:, k, :], self.ng[:, g0 + k:g0 + k + 1], self.rstd[:, :], ALU.mult, ALU.mult,
                          [xT, self.ng, self.rstd], [self.hA])
                return f

            for k in range(KC):
                add(2 + k, mk_sq(k))
                add(3 + k, mk_mm(k))
            add(11, fin)
            for k in range(KC):
                add(12 + k, mk_stt(k))
            return hk

        def norm1a(i):
            P.tag = "norm1"
            self.rms_a(self.xTs[i % 2])

        def norm1b(i):
            l, tt = seq[i]
            P.tag = "norm1"
            self.rms_b(l * 2 * KC, self.hA, self.xTs[i % 2])

        load_resid(0)
        norm1(0)
        for i, (l, tt) in enumerate(seq):
            even = (l % 2 == 0)
            li = l // 2
            t0 = tt * TT
            if tt == 0:
                self.barrier()
                if even:
                    for s_ in range(NS):
                        P.memset(self.VA[s_][:, :, :], 1.0, [self.VA[s_]])
                    P.dma(self.ehg[:, :], ehg_d[li], [ehg_d], [self.ehg])
                    self.even_layer_consts(li)
                else:
                    P.dma(self.ohg[:, :], ohg_d[li], [ohg_d], [self.ohg])
                    self.odd_layer_consts(li)
            self.cur_layer, self.cur_tt, self._blk = l, tt, 0
            self.xT = self.xTs[i % 2]
            self.hT = self.hA
            if even:
                self.even_mixer(li, tt)
                P.tag = "wout"
                self.proj_add(self.ev_w_out, self.ev_w_out[li], self.mixT, KC)
            else:
                self.odd_mixer(li, tt)
                P.tag = "wout"
                self.proj_add(self.od_w_out, self.od_w_out[li], self.mixT, KC)
            nxt = i + 1 < len(seq)
            if nxt:
                load_resid(i + 1)
            P.tag = "norm2"
            self.rmsnorm((l * 2 + 1) * KC, self.hF)
            self.hT = self.hF
            P.tag = "ffn13"
            self.ffn(l, hooks=norm1_hooks(i + 1) if nxt else None)
            P.tag = "resid_io"
            if l < depth - 1:
                P.dma(self.xres[:, t0:t0 + TT].rearrange("(k p) t -> p k t", p=128), self.xT[:, :, :], [self.xT], [self.xres])
            else:
                self.final_out(t0)
        P.op("sp", None, [self.out], [])
        P.emit()

    def barrier(self):
        P = self.P
        f = self.fence
        ps = P.nps()
        P.mm(ps[0:1, 0:2], self.ones_bf[:, 0:1], self.ones_bf[:, 0:2], True, True, [self.ones_bf], [ps])
        P.memset(f["dve"][:, :], 0.0, [f["dve"]], eng="dve")
        P.memset(f["pool"][:, :], 0.0, [f["pool"]], eng="pool")
        P.cp(f["act"][:, :], self.epsc[:, :], [self.epsc], [f["act"]])
        allf = [ps, f["dve"], f["pool"], f["act"]]
        for eng in ("pe", "act", "dve", "pool", "sp"):
            P.op(eng, None, allf, [])

    def final_out(self, t0):
        P = self.P
        self.rmsnorm(2 * self.depth * KC, self.yT)
        for s in range(NS):
            for half in range(2):
                ps = P.nps()
                for q in range(4):
                    k = half * 4 + q
                    P.tr(ps[:, q * 128:(q + 1) * 128], self.yT[:, k, s * 128:(s + 1) * 128], self.ident_f[:, :], [self.yT, self.ident_f], [ps])
                P.cp(self.yout[:, half * 512:(half + 1) * 512], ps[:, :], [ps], [self.yout], eng="dve")
            P.dma(self.out[t0 + s * 128:t0 + (s + 1) * 128, :], self.yout[:, :], [self.yout], [self.out])


def _cols(v):
    v = np.asarray(v, np.float32)
    return np.ascontiguousarray(v.reshape(-1, 128).T)


def prep_inputs(inp, depth, n_even, n_odd):
    f = lambda a: np.ascontiguousarray(np.asarray(a, np.float32))
    ng = [np.zeros((128, 0), np.float32)]
    for l in range(depth):
        ng.append(_cols(inp["mix_norm_g"][l]))
        ng.append(_cols(inp["ffn_norm_g"][l]))
    ng.append(_cols(inp["final_norm_g"]))
    m = {"ng": np.ascontiguousarray(np.concatenate(ng, 1))}
    m["ffn_w1"] = f(inp["ffn_w1"][:depth])
    m["ffn_w3"] = f(inp["ffn_w3"][:depth])
    m["ffn_w2"] = f(inp["ffn_w2"][:depth])
    m["ev_w_in"] = f(inp["ev_w_in"][:n_even])
    m["ev_w_out"] = f(inp["ev_w_out"][:n_even])
    no = max(n_odd, 1)
    m["od_w_in"] = f(inp["od_w_in"][:no])
    m["od_w_out"] = f(inp["od_w_out"][:no])
    m["od_gate_w2"] = f(inp["od_gate_w2"][:no])
    cw = np.asarray(inp["ev_conv_w"], np.float32)[:n_even]
    m["s_cw"] = np.ascontiguousarray(cw.reshape(n_even, 31, 4, 128).transpose(3, 0, 2, 1).reshape(128, -1))
    colsE = lambda a, n: np.ascontiguousarray(np.asarray(a, np.float32)[:n_even].reshape(n_even, n, 128).transpose(2, 0, 1).reshape(128, -1))
    m["s_cb"] = colsE(inp["ev_conv_b"], 4)
    m["s_lng"] = colsE(inp["ev_ln_g"], 4)
    m["s_lnb"] = colsE(inp["ev_ln_b"], 4)
    qw = np.asarray(inp["ev_qk_conv_w"], np.float32)[:n_even]
    m["s_qw"] = np.ascontiguousarray(qw.reshape(n_even, 4, 8, 128).transpose(3, 0, 2, 1).reshape(128, -1))
    m["s_qb"] = colsE(inp["ev_qk_conv_b"], 8)
    gb = np.asarray(inp["ev_gate_b"], np.float32)[:n_even]
    m["s_gbi"] = np.ascontiguousarray(gb[:, 0:4].T)
    m["s_gbf"] = np.ascontiguousarray(gb[:, 4:8].T)
    ogb = np.asarray(inp["od_gate_b"], np.float32)[:no]
    m["s_ogb"] = np.ascontiguousarray(ogb.reshape(no, 4, 128).transpose(2, 0, 1).reshape(128, -1))
    ehg = np.asarray(inp["ev_head_g"], np.float32)[:n_even]
    m["s_ehg"] = np.ascontiguousarray(np.broadcast_to(ehg[:, None, :], (n_even, 128, 512)))
    ohg = np.asarray(inp["od_head_g"], np.float32)[:no]
    m["s_ohg"] = np.ascontiguousarray(np.broadcast_to(ohg[:, None, :], (no, 128, 1024)))
    return m


_CACHE = {}


def run(inp, T_len, depth, ncores):
    key = (T_len, depth)
    if key not in _CACHE:
        _CACHE[key] = Builder(T_len, depth)
    b = _CACHE[key]
    shared = prep_inputs(inp, depth, b.n_even, b.n_odd)
    x = np.asarray(inp["x"], np.float32)
    in_maps = []
    for c in range(ncores):
        mm = dict(shared)
        mm["x"] = np.ascontiguousarray(x[c, :T_len])
        in_maps.append(mm)
    res = run_bass_kernel_spmd(b.nc, in_maps, core_ids=list(range(ncores)))
    return np.stack([r["out"] for r in res.results], 0)


def kernel(**inputs):
    return run(inputs, 4096, 4, 8).astype(np.float32)
```

```python
import math
import numpy as np
import concourse.bass as bass
import concourse.mybir as mybir
from concourse.bass_utils import run_bass_kernel_spmd

F32 = mybir.dt.float32
BF16 = mybir.dt.bfloat16
ALU = mybir.AluOpType
AF = mybir.ActivationFunctionType

D = 1024
KC = 8
TT = 512
NS = 4
FH = 2816
FJ = 22
EPS = 1e-6
SAME_ENGINE_SYNC = True


class T:
    __slots__ = ("name", "ap", "writers", "readers", "dsem", "dcount", "parts")

    def __init__(self, name, ap, parts=None):
        self.name = name
        self.ap = ap
        self.parts = parts
        self.writers = {}
        self.readers = {}
        self.dsem = None
        self.dcount = 0

    def __getitem__(self, k):
        return self.ap[k]


class Op:
    __slots__ = ("idx", "eng", "fn", "deps", "is_dma", "signal", "sem", "val", "dst", "tag", "iname")

    def __init__(self, idx, eng, fn, is_dma):
        self.idx = idx
        self.eng = eng
        self.fn = fn
        self.is_dma = is_dma
        self.deps = []
        self.signal = False
        self.sem = None
        self.val = 0
        self.dst = None
        self.tag = ""
        self.iname = None


class Prog:
    ENGS = ("pe", "act", "dve", "pool", "sp")

    def __init__(self, nc):
        self.nc = nc
        self.ops = []
        self._psi = 0
        self.psb = []
        self.tag = ""

    def sb(self, name, shape, dtype=F32):
        return T(name, self.nc.alloc_sbuf_tensor(name, list(shape), dtype).ap())

    def dram(self, name, shape, dtype, kind="Internal"):
        return T(name, self.nc.dram_tensor(name, list(shape), dtype, kind=kind).ap())

    def init_psum(self):
        self.psb = [T("ps%d" % i, self.nc.alloc_psum_tensor("ps%d" % i, [128, 512], F32).ap()) for i in range(8)]

    def nps(self):
        t = self.psb[self._psi % 7]
        self._psi += 1
        return t

    def op(self, eng, fn, R=(), W=(), dma=False, nowaw=False, semt=None):
        o = Op(len(self.ops), eng, fn, dma)
        o.tag = self.tag
        deps = {}
        R = [p for t in R for p in (t.parts or (t,))]
        W = [p for t in W for p in (t.parts or (t,))]
        for t in R:
            for w in t.writers.values():
                deps[w.idx] = w
        for t in W:
            if not nowaw:
                for w in t.writers.values():
                    deps[w.idx] = w
            for r in t.readers.values():
                deps[r.idx] = r
        for d in deps.values():
            if (not d.is_dma) and (not dma) and d.eng == eng:
                if eng == "pe" or not SAME_ENGINE_SYNC:
                    continue
            o.deps.append(d)
        key = ("dma", o.idx) if dma else eng
        if fn is not None:
            for t in R:
                t.readers[key] = o
            for t in W:
                if nowaw:
                    t.writers[key] = o
                else:
                    t.writers = {key: o}
                t.readers = {}
        if dma:
            assert len(W) == 1
            o.dst = semt or W[0]
        self.ops.append(o)
        return o

    def mm(self, out, lhsT, rhs, start, stop, R, W):
        self.op("pe", lambda e: e.matmul(out, lhsT=lhsT, rhs=rhs, start=start, stop=stop), R, W)

    def tr(self, out, in_, ident, R, W):
        self.op("pe", lambda e: e.transpose(out=out, in_=in_, identity=ident), R, W)

    def act(self, out, in_, func, R, W, scale=1.0, bias=0.0, accum=None):
        self.op("act", lambda e: e.activation(out=out, in_=in_, func=func, bias=bias, scale=scale, accum_out=accum), R, W)

    def tt(self, out, in0, in1, op, R, W, eng="dve"):
        self.op(eng, lambda e: e.tensor_tensor(out=out, in0=in0, in1=in1, op=op), R, W)

    def ts(self, out, in0, s1, op0, R, W, s2=None, op1=None, eng="dve"):
        if op1 is None:
            self.op(eng, lambda e: e.tensor_scalar(out=out, in0=in0, scalar1=s1, scalar2=None, op0=op0), R, W)
        else:
            self.op(eng, lambda e: e.tensor_scalar(out=out, in0=in0, scalar1=s1, scalar2=s2, op0=op0, op1=op1), R, W)

    def stt(self, out, in0, scalar, in1, op0, op1, R, W):
        self.op("dve", lambda e: e.scalar_tensor_tensor(out=out, in0=in0, scalar=scalar, in1=in1, op0=op0, op1=op1), R, W)

    def cp(self, out, in_, R, W, eng="act"):
        if eng == "act":
            self.op("act", lambda e: e.copy(out=out, in_=in_), R, W)
        else:
            self.op(eng, lambda e: e.tensor_copy(out=out, in_=in_), R, W)

    def memset(self, ap, val, W, eng="pool"):
        self.op(eng, lambda e: e.memset(ap, val), (), W)

    def dma(self, out, in_, R, W, eng="sp", nowaw=False, semt=None):
        self.op(eng, lambda e: e.dma_start(out=out, in_=in_), R, W, dma=True, nowaw=nowaw, semt=semt)

    def emit(self):
        nc = self.nc
        for o in self.ops:
            for d in o.deps:
                d.signal = True
        esem = {e: nc.alloc_semaphore("sem_" + e) for e in self.ENGS}
        cnt = {e: 0 for e in self.ENGS}
        nsem = len(esem)
        for o in self.ops:
            if o.is_dma:
                t = o.dst
                if t.dsem is None:
                    t.dsem = nc.alloc_semaphore("dsem_%d" % nsem)
                    nsem += 1
                t.dcount += 1
                o.sem = t.dsem
                o.val = 16 * t.dcount
                o.signal = True
            elif o.signal:
                cnt[o.eng] += 1
                o.sem = esem[o.eng]
                o.val = cnt[o.eng]
        self.nsem = nsem
        streams = {e: [o for o in self.ops if o.eng == e] for e in self.ENGS}

        def run(ename):
            def body(eng):
                known = {}
                for o in streams[ename]:
                    need = {}
                    for d in o.deps:
                        k = id(d.sem)
                        if d.val > known.get(k, 0) and d.val > need.get(k, (None, 0))[1]:
                            need[k] = (d.sem, d.val)
                    for k, (sm, v) in need.items():
                        eng.wait_ge(sm, v)
                        known[k] = v
                    if o.fn is None:
                        continue
                    ins = o.fn(eng)
                    try:
                        o.iname = ins.ins.name
                    except Exception:
                        pass
                    if o.signal:
                        ins.then_inc(o.sem, 16 if o.is_dma else 1)
            return body

        with nc.Block() as block:
            block.tensor(run("pe"))
            block.scalar(run("act"))
            block.vector(run("dve"))
            block.gpsimd(run("pool"))
            block.sync(run("sp"))


class Builder:
    def __init__(self, T_len, depth):
        self.Tn = T_len
        self.depth = depth
        self.NT = T_len // TT
        self.NCH = T_len // 128
        self.n_even = (depth + 1) // 2
        self.n_odd = depth // 2
        self.nc = bass.Bass("TRN2", target_bir_lowering=False)
        self.P = Prog(self.nc)
        self.build()

    def load_w(self, src_t, src_ap, a, b):
        P = self.P
        i = self._wi
        self._wi += 1
        wb = self.wbf[i % len(self.wbf)]
        n = a * b
        wv = wb[:, 0:n].rearrange("p (a b) -> p a b", a=a)
        blk = self._blk
        self._blk += 1
        cache = self.wcache[self.cur_layer]
        if self.cur_tt == 0:
            st = self.wst[self._si % len(self.wst)]
            self._si += 1
            sv = st[:, 0:n].rearrange("p (a b) -> p a b", a=a)
            P.dma(sv, src_ap, [src_t], [st])
            P.cp(wb[:, 0:n], st[:, 0:n], [st], [wb], eng="pool")
            P.dma(cache[blk, :, 0:n], wb[:, 0:n], [wb], [cache], eng="pool", nowaw=True, semt=self.wcs[i % len(self.wbf)])
        else:
            P.dma(wb[:, 0:n], cache[blk, :, 0:n], [cache], [wb])
        return wb, wv

    def wcols(self, src_t, mat_ap, c0, n):
        return self.load_w(src_t, mat_ap[:, c0:c0 + n].rearrange("(k p) n -> p k n", p=128), KC, n)

    def mm_fm(self, ps, M, wb, wv, c0, rhs_t, nk=KC, ncol=TT):
        for k in range(nk):
            self.P.mm(ps[0:M, 0:ncol], wv[:, k, c0:c0 + M], rhs_t[:, k, 0:ncol], k == 0, k == nk - 1, [wb, rhs_t], [ps])

    def mm_tm(self, ps, s, wb, wv, c0, N, lhs_t):
        for k in range(KC):
            self.P.mm(ps[:, 0:N], lhs_t[:, k, s * 128:(s + 1) * 128], wv[:, k, c0:c0 + N], k == 0, k == KC - 1, [wb, lhs_t], [ps])

    def rms_a(self, xT):
        P = self.P
        bufs = [self.mean, self.tmpB]
        for k in range(KC):
            b = bufs[k % 2]
            P.act(b[:, :], xT[:, k, :], AF.Square, [xT], [b])
            if k == 1:
                P.tt(self.rstd[:, :], bufs[0][:, :], bufs[1][:, :], ALU.add, [bufs[0], bufs[1]], [self.rstd])
            elif k > 1:
                P.tt(self.rstd[:, :], self.rstd[:, :], b[:, :], ALU.add, [self.rstd, b], [self.rstd])

    def rms_b(self, gcol0, out_t, xT):
        P = self.P
        ps = P.nps()
        P.mm(ps[:, :], self.ones_f[:, :], self.rstd[:, :], True, True, [self.rstd, self.ones_f], [ps])
        P.act(self.rstd[:, :], ps[:, :], AF.Sqrt, [ps], [self.rstd], scale=1.0 / D, bias=self.epsc[:, 0:1])
        P.op("dve", lambda e: e.reciprocal(out=self.rstd[:, :], in_=self.rstd[:, :]), [self.rstd], [self.rstd])
        for k in range(KC):
            P.stt(out_t[:, k, :], xT[:, k, :], self.ng[:, gcol0 + k:gcol0 + k + 1], self.rstd[:, :], ALU.mult, ALU.mult,
                  [xT, self.ng, self.rstd], [out_t])

    def rmsnorm(self, gcol0, out_t, xT=None):
        P = self.P
        xT = xT or self.xT
        ps = P.nps()
        sqs = [self.sq, self.sq2]
        for k in range(KC):
            sq = sqs[k % 2]
            P.act(sq[:, :], xT[:, k, :], AF.Square, [xT], [sq])
            P.mm(ps[:, :], self.ones_bf[:, :], sq[:, :], k == 0, k == KC - 1, [sq, self.ones_bf], [ps])
        P.act(self.rstd[:, :], ps[:, :], AF.Sqrt, [ps], [self.rstd], scale=1.0 / D, bias=self.epsc[:, 0:1])
        P.op("dve", lambda e: e.reciprocal(out=self.rstd[:, :], in_=self.rstd[:, :]), [self.rstd], [self.rstd])
        for k in range(KC):
            P.stt(out_t[:, k, :], xT[:, k, :], self.ng[:, gcol0 + k:gcol0 + k + 1], self.rstd[:, :], ALU.mult, ALU.mult,
                  [xT, self.ng, self.rstd], [out_t])

    def proj_add(self, src_t, mat_ap, in_t, nk):
        P = self.P
        if nk <= 8:
            for cb in range(D // 256):
                wb, wv = self.load_w(src_t, mat_ap[:, cb * 256:(cb + 1) * 256].rearrange("(k p) n -> p k n", p=128), nk, 256)
                for mi in range(2):
                    m = cb * 2 + mi
                    ps = P.nps()
                    self.mm_fm(ps, 128, wb, wv, mi * 128, in_t, nk=nk)
                    P.tt(self.xT[:, m, :], self.xT[:, m, :], ps[:, :], ALU.add, [self.xT, ps], [self.xT])
        else:
            hk = nk // 2
            for m in range(KC):
                ps = P.nps()
                for half in range(2):
                    wb, wv = self.load_w(src_t, mat_ap[half * hk * 128:(half + 1) * hk * 128, m * 128:(m + 1) * 128].rearrange("(k p) n -> p k n", p=128), hk, 128)
                    for k in range(hk):
                        kk = half * hk + k
                        P.mm(ps[:, :], wv[:, k, :], in_t[:, kk, :], kk == 0, kk == nk - 1, [wb, in_t], [ps])
                P.tt(self.xT[:, m, :], self.xT[:, m, :], ps[:, :], ALU.add, [self.xT, ps], [self.xT])

    def ffn(self, l, hooks=None):
        P = self.P
        for cb in range(0, FH, 256):
            n = 256
            wb1, wv1 = self.wcols(self.w1, self.w1[l], cb, n)
            wb3, wv3 = self.wcols(self.w3, self.w3[l], cb, n)
            for ji in range(n // 128):
                j = cb // 128 + ji
                p1 = P.nps()
                p3 = P.nps()
                self.mm_fm(p1, 128, wb1, wv1, ji * 128, self.hT)
                self.mm_fm(p3, 128, wb3, wv3, ji * 128, self.hT)
                P.act(self.tmpA[:, :], p1[:, :], AF.Silu, [p1], [self.tmpA])
                P.tt(self.actT[:, j, :], self.tmpA[:, :], p3[:, :], ALU.mult, [self.tmpA, p3], [self.actT])
                if hooks and j in hooks:
                    tg = P.tag
                    for fn_ in hooks[j]:
                        fn_()
                    P.tag = tg
        P.tag = "ffn2"
        self.proj_add(self.w2, self.w2[l], self.actT, FJ)

    def even_layer_consts(self, e):
        P = self.P
        for c8 in range(8):
            for j in range(4):
                col = ((e * 8 + c8) * 4 + j)
                P.ts(self.dgq[:, c8, j, :], self.ident_f[:, :], self.qw[:, col:col + 1], ALU.mult, [self.ident_f, self.qw], [self.dgq], eng="pool")
        P.ts(self.nbf[:, 0:1], self.gbf[:, e:e + 1], -1.0, ALU.mult, [self.gbf], [self.nbf])
        for h in range(4):
            P.memset(self.CH[h][:, :], 0.0, [self.CH[h]])
            P.memset(self.CB[h][:, :], 0.0, [self.CB[h]])
        P.memset(self.DL[:, :], 0.0, [self.DL])
        P.memset(self.Blast[:, :], 0.0, [self.Blast])
        P.memset(self.Mlast[:, :], 0.0, [self.Mlast])
        P.memset(self.E2L[:, :, :], 1.0, [self.E2L])

    def even_mixer(self, e, tt):
        P = self.P
        win = self.ev_w_in
        W = self.ev_w_in[e]
        first = (tt == 0)
        P.tag = "ev_proj"
        for cc in range(4):
            if cc % 2 == 0:
                wba, wva = self.wcols(win, W, cc * 128, 256)
                wbg, wvg = self.wcols(win, W, 512 + cc * 128, 256)
            if first:
                P.memset(self.UB[cc][:, 0:30], 0.0, [self.UB[cc]])
            else:
                P.cp(self.UB[cc][:, 0:30], self.UB[cc][:, TT:TT + 30], [self.UB[cc]], [self.UB[cc]], eng="pool")
            pa = P.nps()
            pg = P.nps()
            self.mm_fm(pa, 128, wba, wva, (cc % 2) * 128, self.hT)
            self.mm_fm(pg, 128, wbg, wvg, (cc % 2) * 128, self.hT)
            P.act(self.tmpA[:, :], pg[:, :], AF.Sigmoid, [pg], [self.tmpA])
            P.tt(self.UB[cc][:, 30:30 + TT], pa[:, :], self.tmpA[:, :], ALU.mult, [pa, self.tmpA], [self.UB[cc]])
        for c8 in range(8):
                if c8 % 2 == 0:
                    wb, wv = self.wcols(win, W, 1024 + c8 * 128, 256)
                if first:
                    P.memset(self.QKB[c8][:, 0:3], 0.0, [self.QKB[c8]])
                else:
                    P.cp(self.QKB[c8][:, 0:3], self.QKB[c8][:, TT:TT + 3], [self.QKB[c8]], [self.QKB[c8]], eng="pool")
                ps = P.nps()
                self.mm_fm(ps, 128, wb, wv, (c8 % 2) * 128, self.hT)
                P.cp(self.QKB[c8][:, 3:3 + TT], ps[:, :], [ps], [self.QKB[c8]])
        for half in range(2):
            wb, wv = self.wcols(win, W, 2048 + half * 256, 256)
            for s in range(NS):
                ps = P.nps()
                self.mm_tm(ps, s, wb, wv, 0, 256, self.hT)
                P.cp(self.VA[s][:, half * 2:half * 2 + 2, 0:128], ps[:, 0:256].rearrange("p (h e) -> p h e", h=2), [ps], [self.VA[s]])
        for half in range(2):
            wb, wv = self.wcols(win, W, 2560 + half * 256, 256)
            for s in range(NS):
                ps = P.nps()
                self.mm_tm(ps, s, wb, wv, 0, 256, self.hT)
                P.act(self.SG[s][:, half * 256:(half + 1) * 256], ps[:, 0:256], AF.Sigmoid, [ps], [self.SG[s]])
        P.tag = "ev_qkconv"
        for c8 in range(8):
            ps = P.nps()
            for j in range(4):
                P.mm(ps[:, :], self.dgq[:, c8, j, :], self.QKB[c8][:, j:j + TT], j == 0, j == 3, [self.dgq, self.QKB[c8]], [ps])
            col = e * 8 + c8
            P.act(self.QK[c8][:, :], ps[:, :], AF.Silu, [ps, self.qb], [self.QK[c8]], bias=self.qb[:, col:col + 1])
        P.tag = "ev_gates"
        wb, wv = self.wcols(win, W, 3072, 8)
        pgi = P.nps()
        pgf = P.nps()
        self.mm_fm(pgi, 4, wb, wv, 0, self.hT)
        self.mm_fm(pgf, 4, wb, wv, 4, self.hT)
        g = self.G
        P.act(g["ig"][:, :], pgi[0:4, :], AF.Identity, [pgi, self.gbi], [g["ig"]], bias=self.gbi[:, e:e + 1])
        P.act(g["sp"][:, :], pgf[0:4, :], AF.Exp, [pgf, self.nbf], [g["sp"]], scale=-1.0, bias=self.nbf[:, 0:1])
        P.act(g["sp"][:, :], g["sp"][:, :], AF.Ln, [g["sp"]], [g["sp"]], bias=1.0)
        P.op("dve", lambda en: en.tensor_tensor_scan(out=g["B"][:, :], data0=self.RM[0:4, :], data1=g["sp"][:, :], initial=0.0, op0=ALU.mult, op1=ALU.add),
             [self.RM, g["sp"]], [g["B"]])
        P.tt(g["a"][:, :], g["ig"][:, :], g["B"][:, :], ALU.add, [g["ig"], g["B"]], [g["a"]])
        P.memset(self.DL[:, :], 0.0, [self.DL])
        Bv = g["B"][:, :].rearrange("p (c l) -> p c l", l=128)
        DLv = self.DL[:, :].rearrange("p (c l) -> p c l", l=128)
        P.ts(DLv[:, 1:NS, 0:1], Bv[:, 0:NS - 1, 127:128], -1.0, ALU.mult, [g["B"]], [self.DL])
        P.ts(self.DL[:, 0:1], self.Blast[:, 0:1], -1.0, ALU.mult, [self.Blast], [self.DL])
        P.op("dve", lambda en: en.tensor_tensor_scan(out=g["M"][:, :], data0=self.DL[:, :], data1=g["a"][:, :], initial=self.Mlast[:, 0:1], op0=ALU.add, op1=ALU.max),
             [self.DL, g["a"], self.Mlast], [g["M"]])
        Mv = g["M"][:, :].rearrange("p (c l) -> p c l", l=128)
        P.cp(g["msh"][:, 1:NS], Mv[:, 0:NS - 1, 127], [g["M"]], [g["msh"]], eng="dve")
        P.cp(g["msh"][:, 0:1], self.Mlast[:, 0:1], [self.Mlast], [g["msh"]], eng="dve")
        P.tt(g["mp"][:, :], g["msh"][:, :], DLv[:, :, 0], ALU.add, [g["msh"], self.DL], [g["mp"]])
        P.ts(g["nmp"][:, :], g["mp"][:, :], -1.0, ALU.mult, [g["mp"]], [g["nmp"]], s2=math.log(128.0 ** -0.5), op1=ALU.add)
        P.tt(g["bm"][:, :], g["B"][:, :], g["M"][:, :], ALU.subtract, [g["B"], g["M"]], [g["bm"]])
        for c in range(NS):
            sl = slice(c * 128, (c + 1) * 128)
            P.act(g["E1"][:, sl], g["a"][:, sl], AF.Exp, [g["a"], g["nmp"]], [g["E1"]], bias=g["nmp"][:, c:c + 1])
            P.act(g["E2"][:, sl], g["M"][:, sl], AF.Exp, [g["M"], g["mp"]], [g["E2"]], scale=-1.0, bias=g["mp"][:, c:c + 1])
        P.act(g["E3"][:, :], g["bm"][:, :], AF.Exp, [g["bm"]], [g["E3"]])
        P.cp(self.Blast[:, 0:1], g["B"][:, TT - 1:TT], [g["B"]], [self.Blast], eng="dve")
        P.cp(self.Mlast[:, 0:1], g["M"][:, TT - 1:TT], [g["M"]], [self.Mlast], eng="dve")
        P.tag = "ev_conv"
        ps1 = P.nps()
        ps2 = P.nps()
        for cc in range(4):
            ps = P.nps()
            base = (e * 4 + cc) * 31
            for dgt, j0, nj in ((self.dgA, 0, 16), (self.dgB, 16, 15)):
                P.tt(dgt[:, 0:nj, :], self.ident_f[:, :].unsqueeze(1).to_broadcast([128, nj, 128]),
                     self.cw[:, base + j0:base + j0 + nj].unsqueeze(2).to_broadcast([128, nj, 128]), ALU.mult, [self.ident_f, self.cw], [dgt], eng="pool")
            for j in range(31):
                dgt, jj = (self.dgA, j) if j < 16 else (self.dgB, j - 16)
                P.mm(ps[:, :], dgt[:, jj, :], self.UB[cc][:, j:j + TT], j == 0, j == 30, [dgt, self.UB[cc]], [ps])
            col = e * 4 + cc
            P.act(self.Y[cc][:, :], ps[:, :], AF.Identity, [ps, self.cb], [self.Y[cc]], bias=self.cb[:, col:col + 1])
            P.act(self.tmpA[:, :], self.Y[cc][:, :], AF.Square, [self.Y[cc]], [self.tmpA])
            P.mm(ps1[:, :], self.ones_f[:, :], self.Y[cc][:, :], cc == 0, cc == 3, [self.ones_f, self.Y[cc]], [ps1])
            P.mm(ps2[:, :], self.ones_f[:, :], self.tmpA[:, :], cc == 0, cc == 3, [self.ones_f, self.tmpA], [ps2])
        P.ts(self.mean[:, :], ps1[:, :], 1.0 / 512, ALU.mult, [ps1], [self.mean])
        P.tt(self.tmpB[:, :], self.mean[:, :], self.mean[:, :], ALU.mult, [self.mean], [self.tmpB])
        P.stt(self.rstd[:, :], ps2[:, :], 1.0 / 512, self.tmpB[:, :], ALU.mult, ALU.subtract, [ps2, self.tmpB], [self.rstd])
        P.act(self.rstd[:, :], self.rstd[:, :], AF.Sqrt, [self.rstd], [self.rstd], bias=self.epsc[:, 0:1])
        P.op("dve", lambda en: en.reciprocal(out=self.rstd[:, :], in_=self.rstd[:, :]), [self.rstd], [self.rstd])
        for cc in range(4):
            col = e * 4 + cc
            P.tt(self.tmpB[:, :], self.Y[cc][:, :], self.mean[:, :], ALU.subtract, [self.Y[cc], self.mean], [self.tmpB])
            P.tt(self.tmpB[:, :], self.tmpB[:, :], self.rstd[:, :], ALU.mult, [self.tmpB, self.rstd], [self.tmpB])
            P.act(self.mixT[:, cc, :], self.tmpB[:, :], AF.Silu, [self.tmpB, self.lng, self.lnb], [self.mixT],
                  scale=self.lng[:, col:col + 1], bias=self.lnb[:, col:col + 1])
        P.tag = "ev_gates2"
        pst = P.nps()
        for s in range(NS):
            sl = slice(s * 128, (s + 1) * 128)
            for qi, nm in enumerate(("E1", "E2", "E3")):
                P.mm(pst[:, s * 12 + qi * 4: s * 12 + qi * 4 + 4], g[nm][0:4, sl], self.ident_f[0:4, 0:4], True, True, [g[nm], self.ident_f], [pst])
        P.cp(self.ETM[:, :, :], pst[:, 0:NS * 12].rearrange("p (s q) -> p s q", s=NS), [pst], [self.ETM], eng="dve")
        E2v = g["E2"][:, :].rearrange("p (c l) -> p c l", l=128)
        P.tt(g["e2m"][:, :, :], E2v[:, :, 127].unsqueeze(1).to_broadcast([4, 4, NS]), self.BM[:, :, :], ALU.mult, [g["E2"], self.BM], [g["e2m"]])
        psb = P.nps()
        P.mm(psb[:, 0:4 * NS], self.ones_f[0:4, :], g["e2m"][:, :, :].rearrange("p h c -> p (h c)"), True, True, [g["e2m"], self.ones_f], [psb])
        P.cp(self.E2L[:, :, 1 + tt * NS:1 + (tt + 1) * NS], psb[:, 0:4 * NS].rearrange("p (h c) -> p h c", h=4), [psb], [self.E2L], eng="dve")
        P.tag = "mlstm"
        pb = P.psb

        def st1(s):
            sl = slice(s * 128, (s + 1) * 128)
            pS = pb[s % 2]
            for h in range(4):
                P.mm(pS[:, h * 128:(h + 1) * 128], self.QK[4 + h][:, sl], self.QK[h][:, sl], True, True, [self.QK[4 + h], self.QK[h]], [pS])
            pK = pb[2]
            pKb = pK[:, :].bitcast(BF16)
            for h in range(4):
                P.tr(pKb[:, h * 128:(h + 1) * 128], self.QK[4 + h][:, sl], self.ident_b[:, :], [self.QK[4 + h], self.ident_b], [pK])
            for h in range(4):
                e1 = self.ETM[:, s, h:h + 1]
                P.stt(self.PT[h][:, :], pS[:, h * 128:(h + 1) * 128], e1, self.CMASK[:, :], ALU.mult, ALU.mult, [pS, self.ETM, self.CMASK], [self.PT[h]])
                P.act(self.KH[h][:, :], pKb[:, h * 128:(h + 1) * 128], AF.Identity, [pK, self.ETM], [self.KH[h]], scale=e1)

        def st2(s):
            gch = tt * NS + s
            sl = slice(s * 128, (s + 1) * 128)
            pO = [pb[4], pb[5]]
            for h in range(4):
                po = pO[h // 2][:, (h % 2) * 129:(h % 2) * 129 + 129]
                va = self.VA[s][:, h, 0:129]
                P.mm(po, self.PT[h][:, :], va, True, False, [self.PT[h], self.VA[s]], [pO[h // 2]])
                P.mm(po, self.QK[h][:, sl], self.CB[h][:, 0:129], False, True, [self.QK[h], self.CB[h]], [pO[h // 2]])
                pU = pb[6 + h % 2]
                P.mm(pU[:, 0:129], self.KH[h][:, :], va, True, True, [self.KH[h], self.VA[s]], [pU])
                P.stt(self.CH[h][:, :], self.CH[h][:, :], self.E2L[:, h, gch:gch + 1], pU[:, 0:129], ALU.mult, ALU.add, [self.CH[h], self.E2L, pU], [self.CH[h]])
                P.act(self.CB[h][:, 0:129], self.CH[h][:, :], AF.Identity, [self.CH[h], self.E2L], [self.CB[h]], scale=self.E2L[:, h, gch + 1:gch + 2])
            return pO

        def st3(s, pO):
            sl = slice(s * 128, (s + 1) * 128)
            for hp in range(2):
                den = pO[hp][:, 0:258].rearrange("p (h e) -> p h e", h=2)[:, :, 128]
                P.tt(self.R[:, hp * 2:hp * 2 + 2], den, self.ETM[:, s, 4 + hp * 2:6 + hp * 2], ALU.mult, [pO[hp], self.ETM], [self.R])
            P.stt(self.R2[:, :], self.R[:, :], -1.0, self.R[:, :], ALU.mult, ALU.max, [self.R], [self.R2])
            P.tt(self.R[:, :], self.R2[:, :], self.ETM[:, s, 8:12], ALU.max, [self.R2, self.ETM], [self.R])
            P.op("dve", lambda en: en.reciprocal(out=self.R[:, :], in_=self.R[:, :]), [self.R], [self.R])
            P.tt(self.R[:, :], self.R[:, :], self.ETM[:, s, 4:8], ALU.mult, [self.R, self.ETM], [self.R])
            for h in range(4):
                po = pO[h // 2][:, (h % 2) * 129:(h % 2) * 129 + 128]
                P.stt(self.HH[h][:, :], po, self.R[:, h:h + 1], self.SG[s][:, h * 128:(h + 1) * 128], ALU.mult, ALU.mult, [pO[h // 2], self.R, self.SG[s]], [self.HH[h]])
                P.act(self.junk[:, 0:128], self.HH[h][:, :], AF.Square, [self.HH[h]], [self.junk, self.SS], accum=self.SS[:, h:h + 1])
            P.act(self.RS[:, :], self.SS[:, :], AF.Sqrt, [self.SS], [self.RS], scale=1.0 / 128, bias=self.epsc[:, 0:1])
            P.op("dve", lambda en: en.reciprocal(out=self.RS[:, :], in_=self.RS[:, :]), [self.RS], [self.RS])
            pT = pb[3]
            pTb = pT[:, :].bitcast(BF16)
            for h in range(4):
                P.stt(self.HM[:, h * 128:(h + 1) * 128], self.HH[h][:, :], self.RS[:, h:h + 1], self.ehg[:, h * 128:(h + 1) * 128], ALU.mult, ALU.mult,
                      [self.HH[h], self.RS, self.ehg], [self.HM])
                P.tr(pTb[:, h * 128:(h + 1) * 128], self.HM[:, h * 128:(h + 1) * 128], self.ident_b[:, :], [self.HM, self.ident_b], [pT])
            P.cp(self.mixT[:, 4:8, sl], pTb[:, 0:512].rearrange("p (h t) -> p h t", h=4), [pT], [self.mixT])

        st1(0)
        for s in range(NS):
            pO = st2(s)
            if s + 1 < NS:
                st1(s + 1)
            st3(s, pO)

    def odd_layer_consts(self, o):
        P = self.P
        for h in range(4):
            P.memset(self.SH[h][:, :], 0.0, [self.SH[h]])
            P.memset(self.SB[h][:, :], 0.0, [self.SB[h]])
            P.memset(self.EA[h][:, :], 1.0, [self.EA[h]])
        P.ts(self.ngb[:, :], self.ogb[:, o * 4:o * 4 + 4], -1.0, ALU.mult, [self.ogb], [self.ngb])
        P.dma(self.w2g_st[:, :], self.od_gate_w2[o], [self.od_gate_w2], [self.w2g_st])
        P.cp(self.w2g[:, :], self.w2g_st[:, :], [self.w2g_st], [self.w2g], eng="pool")

    def odd_mixer(self, o, tt):
        P = self.P
        win = self.od_w_in
        W = self.od_w_in[o]
        P.tag = "od_proj"
        wb, wv = self.wcols(win, W, 3072, 16)
        pg = P.nps()
        self.mm_fm(pg, 16, wb, wv, 0, self.hT)
        P.cp(self.GLB[:, :], pg[0:16, :], [pg], [self.GLB])
        for h in range(4):
            if h % 2 == 0:
                wbq, wvq = self.wcols(win, W, h * 128, 256)
                wbk, wvk = self.wcols(win, W, 512 + h * 128, 256)
            pl = P.nps()
            P.mm(pl[:, :], self.w2g[0:16, h * 128:(h + 1) * 128], self.GLB[0:16, :], True, True, [self.w2g, self.GLB], [pl])
            P.act(self.tmpA[:, :], pl[:, :], AF.Exp, [pl, self.ngb], [self.tmpA], scale=-1.0, bias=self.ngb[:, h:h + 1])
            P.act(self.tmpA[:, :], self.tmpA[:, :], AF.Ln, [self.tmpA], [self.tmpA], bias=1.0)
            P.op("dve", lambda en: en.tensor_tensor_scan(out=self.tmpB[:, :], data0=self.RM[:, :], data1=self.tmpA[:, :], initial=0.0, op0=ALU.mult, op1=ALU.add),
                 [self.RM, self.tmpA], [self.tmpB])
            P.act(self.EBQ[:, :], self.tmpB[:, :], AF.Exp, [self.tmpB], [self.EBQ], scale=-1.0 / 16)
            P.act(self.EBK[:, :], self.tmpB[:, :], AF.Exp, [self.tmpB], [self.EBK], scale=1.0 / 16)
            pq = P.nps()
            pk = P.nps()
            self.mm_fm(pq, 128, wbq, wvq, (h % 2) * 128, self.hT)
            self.mm_fm(pk, 128, wbk, wvk, (h % 2) * 128, self.hT)
            P.stt(self.QK[h][:, :], pq[:, :], 128.0 ** -0.5, self.EBQ[:, :], ALU.mult, ALU.mult, [pq, self.EBQ], [self.QK[h]])
            P.tt(self.QK[4 + h][:, :], pk[:, :], self.EBK[:, :], ALU.mult, [pk, self.EBK], [self.QK[4 + h]])
            EBv = self.EBQ[:, :].rearrange("p (c l) -> p c l", l=128)
            P.cp(self.EA[h][:, 1 + tt * NS:1 + (tt + 1) * NS], EBv[:, :, 127], [self.EBQ], [self.EA[h]], eng="dve")
        for q4 in range(4):
            wb, wv = self.wcols(win, W, 1024 + q4 * 256, 256)
            for s in range(NS):
                ps = P.nps()
                self.mm_tm(ps, s, wb, wv, 0, 256, self.hT)
                P.cp(self.VG[s][:, q4 * 256:(q4 + 1) * 256], ps[:, 0:256], [ps], [self.VG[s]])
        for q4 in range(4):
            wb, wv = self.wcols(win, W, 2048 + q4 * 256, 256)
            for s in range(NS):
                ps = P.nps()
                self.mm_tm(ps, s, wb, wv, 0, 256, self.hT)
                P.act(self.tmpA[:, 0:256], ps[:, 0:256], AF.Silu, [ps], [self.tmpA])
                P.tt(self.GR[s][:, q4 * 256:(q4 + 1) * 256], self.tmpA[:, 0:256], self.ohg[:, q4 * 256:(q4 + 1) * 256], ALU.mult, [self.tmpA, self.ohg], [self.GR[s]], eng="pool")
        P.tag = "gla"
        pb = P.psb

        def st1(s):
            sl = slice(s * 128, (s + 1) * 128)
            pA = pb[s % 2]
            for h in range(4):
                P.mm(pA[:, h * 128:(h + 1) * 128], self.QK[4 + h][:, sl], self.QK[h][:, sl], True, True, [self.QK[4 + h], self.QK[h]], [pA])
            pK = pb[2]
            pKb = pK[:, :].bitcast(BF16)
            for h in range(4):
                P.tr(pKb[:, h * 128:(h + 1) * 128], self.QK[4 + h][:, sl], self.ident_b[:, :], [self.QK[4 + h], self.ident_b], [pK])
            P.tt(self.AT[:, :], pA[:, :], self.CMASK4[:, :], ALU.mult, [pA, self.CMASK4], [self.AT])
            P.cp(self.KTM[:, :], pKb[:, 0:512], [pK], [self.KTM])

        def st2(s):
            gch = tt * NS + s
            sl = slice(s * 128, (s + 1) * 128)
            pO = [pb[4], pb[5]]
            for h in range(4):
                po = pO[h // 2][:, (h % 2) * 256:(h % 2) * 256 + 256]
                vh = self.VG[s][:, h * 256:(h + 1) * 256]
                P.mm(po, self.AT[:, h * 128:(h + 1) * 128], vh, True, False, [self.AT, self.VG[s]], [pO[h // 2]])
                P.mm(po, self.QK[h][:, sl], self.SB[h][:, :], False, True, [self.QK[h], self.SB[h]], [pO[h // 2]])
                pU = pb[6 + h % 2]
                P.mm(pU[:, 0:256], self.KTM[:, h * 128:(h + 1) * 128], vh, True, True, [self.KTM, self.VG[s]], [pU])
                P.stt(self.SH[h][:, :], self.SH[h][:, :], self.EA[h][:, gch:gch + 1], pU[:, 0:256], ALU.mult, ALU.add, [self.SH[h], self.EA[h], pU], [self.SH[h]])
                P.act(self.SB[h][:, :], self.SH[h][:, :], AF.Identity, [self.SH[h], self.EA[h]], [self.SB[h]], scale=self.EA[h][:, gch + 1:gch + 2])
                P.act(self.junk[:, 0:256], po, AF.Square, [pO[h // 2]], [self.junk, self.SS], accum=self.SS[:, h:h + 1])
            return pO

        def st3(s, pO):
            sl = slice(s * 128, (s + 1) * 128)
            P.act(self.RS[:, :], self.SS[:, :], AF.Sqrt, [self.SS], [self.RS], scale=1.0 / 256, bias=self.epsc[:, 0:1])
            P.op("dve", lambda en: en.reciprocal(out=self.RS[:, :], in_=self.RS[:, :]), [self.RS], [self.RS])
            pT = pb[3]
            pTb = pT[:, :].bitcast(BF16)
            for h in range(4):
                po = pO[h // 2][:, (h % 2) * 256:(h % 2) * 256 + 256]
                P.stt(self.OM[:, h * 256:(h + 1) * 256], po, self.RS[:, h:h + 1], self.GR[s][:, h * 256:(h + 1) * 256], ALU.mult, ALU.mult,
                      [pO[h // 2], self.RS, self.GR[s]], [self.OM])
                for q in range(2):
                    c = h * 2 + q
                    P.tr(pTb[:, c * 128:(c + 1) * 128], self.OM[:, c * 128:(c + 1) * 128], self.ident_b[:, :], [self.OM, self.ident_b], [pT])
            P.cp(self.mixT[:, 0:8, sl], pTb[:, 0:1024].rearrange("p (h t) -> p h t", h=8), [pT], [self.mixT])

        st1(0)
        for s in range(NS):
            pO = st2(s)
            if s + 1 < NS:
                st1(s + 1)
            st3(s, pO)

    def build(self):
        nc, P = self.nc, self.P
        Tn, depth = self.Tn, self.depth
        ne, no = self.n_even, max(self.n_odd, 1)
        ext = lambda name, shape: P.dram(name, shape, F32, kind="ExternalInput")
        self.x = ext("x", [Tn, D])
        self.ngd = ext("ng", [128, (2 * depth + 1) * KC])
        self.w1 = ext("ffn_w1", [depth, D, FH])
        self.w3 = ext("ffn_w3", [depth, D, FH])
        self.w2 = ext("ffn_w2", [depth, FH, D])
        self.ev_w_in = ext("ev_w_in", [ne, D, 3080])
        self.ev_w_out = ext("ev_w_out", [ne, D, D])
        self.od_w_in = ext("od_w_in", [no, D, 3088])
        self.od_w_out = ext("od_w_out", [no, D, D])
        self.od_gate_w2 = ext("od_gate_w2", [no, 16, 512])
        smalls = {"cw": ne * 4 * 31, "cb": ne * 4, "lng": ne * 4, "lnb": ne * 4, "qw": ne * 8 * 4, "qb": ne * 8, "ogb": no * 4}
        sm_d = {k: ext("s_" + k, [128, n]) for k, n in smalls.items()}
        gbi_d = ext("s_gbi", [4, ne])
        gbf_d = ext("s_gbf", [4, ne])
        ehg_d = ext("s_ehg", [ne, 128, 512])
        ohg_d = ext("s_ohg", [no, 128, 1024])
        self.out = P.dram("out", [Tn, D], F32, kind="ExternalOutput")
        self.xres = P.dram("xres", [D, Tn], F32)
        P.init_psum()
        self.xTs = [P.sb("xT%d" % i, [128, KC, TT]) for i in range(2)]
        self.xT = self.xTs[0]
        self.hA = P.sb("hA", [128, KC, TT], BF16)
        self.hF = P.sb("hF", [128, KC, TT], BF16)
        self.hT = self.hA
        self.mixT = P.sb("mixT", [128, KC, TT], BF16)
        self.actT = P.sb("actT", [128, FJ, TT], BF16)
        actf = self.actT.ap.rearrange("p j t -> p (j t)").bitcast(F32)
        self.yT = T("yT", actf[:, 0:KC * TT].rearrange("p (k t) -> p k t", k=KC), parts=[self.actT])
        self.xin = T("xin", self.mixT.ap.rearrange("p k t -> p (k t)").bitcast(F32)[:, 0:D], parts=[self.mixT])
        self.yout = T("yout", self.hF.ap.rearrange("p k t -> p (k t)").bitcast(F32)[:, 0:D], parts=[self.hF])
        self.wst = [P.sb("wst%d" % i, [128, 2048]) for i in range(2)]
        self.wbf = [P.sb("wbf%d" % i, [128, 2048], BF16) for i in range(6)]
        self._wi = 0
        self._si = 0
        self.wcs = [T("wcs%d" % i, None) for i in range(6)]
        self.wcache = [P.dram("wcache%d" % l, [80, 128, 2048], BF16) for l in range(depth)]
        self.sq = P.sb("sq", [128, TT], BF16)
        self.sq2 = P.sb("sq2", [128, TT], BF16)
        self.rstd = P.sb("rstd", [128, TT])
        self.mean = P.sb("mean", [128, TT])
        self.tmpA = P.sb("tmpA", [128, TT])
        self.tmpB = P.sb("tmpB", [128, TT])
        self.junk = P.sb("junk", [128, 256], BF16)
        self.ng = P.sb("ngs", [128, (2 * depth + 1) * KC])
        self.ident_f = P.sb("ident_f", [128, 128])
        self.ident_b = P.sb("ident_b", [128, 128], BF16)
        self.ones_f = P.sb("ones_f", [128, 128])
        self.ones_bf = P.sb("ones_bf", [128, 128], BF16)
        self.CMASK = P.sb("cmask", [128, 128])
        self.CMASK4 = P.sb("cmask4", [128, 512])
        self.RM = P.sb("rm", [128, TT])
        self.BM = P.sb("bm", [4, 4, NS])
        self.epsc = P.sb("epsc", [128, 1])
        self.fence = {e: P.sb("fence_" + e, [128, 1]) for e in ("act", "dve", "pool")}
        for k, n in smalls.items():
            setattr(self, k, P.sb("sb_" + k, [128, n]))
        self.gbi = P.sb("gbi", [4, ne])
        self.gbf = P.sb("gbf", [4, ne])
        self.nbf = P.sb("nbf", [4, 1])
        self.ehg = P.sb("ehg", [128, 512])
        self.ohg = P.sb("ohg", [128, 1024])
        self.QK = [P.sb("QK%d" % i, [128, TT], BF16) for i in range(8)]
        self.SS = P.sb("SS", [128, 4])
        self.RS = P.sb("RS", [128, 4])
        UNI_BYTES = 56 * 1024
        uni = nc.alloc_sbuf_tensor("uni", [128, UNI_BYTES // 2], BF16).ap()
        self._uoff = 0

        def carve(name, shape, dtype=F32):
            esz = 4 if dtype == F32 else 2
            n = 1
            for d_ in shape[1:]:
                n *= d_
            nb = (n * esz + 31) // 32 * 32
            assert self._uoff + nb <= UNI_BYTES, (name, self._uoff, nb)
            ap = uni[0:shape[0], self._uoff // 2:self._uoff // 2 + n * esz // 2]
            self._uoff += nb
            if dtype == F32:
                ap = ap.bitcast(F32)
            if len(shape) == 3:
                ap = ap.rearrange("p (a b) -> p a b", a=shape[1])
            elif len(shape) == 4:
                ap = ap.rearrange("p (a b c) -> p a b c", a=shape[1], b=shape[2])
            return T(name, ap)

        if ne:
            self._uoff = 0
            self.dgA = carve("dgA", [128, 16, 128], BF16)
            self.dgB = carve("dgB", [128, 15, 128], BF16)
            self.dgq = carve("dgq", [128, 8, 4, 128], BF16)
            self.UB = [carve("UB%d" % i, [128, TT + 30], BF16) for i in range(4)]
            self.QKB = [carve("QKB%d" % i, [128, TT + 4], BF16) for i in range(8)]
            self.VA = [carve("VA%d" % i, [128, 4, 130], BF16) for i in range(NS)]
            self.SG = [carve("SG%d" % i, [128, 512], BF16) for i in range(NS)]
            self.Y = [carve("Y%d" % i, [128, TT]) for i in range(4)]
            gn = ("ig", "sp", "B", "a", "M", "bm", "E1", "E2", "E3", "DL")
            self.G = {n: T("g_" + n, actf[0:4, i * TT:(i + 1) * TT], parts=[self.actT]) for i, n in enumerate(gn)}
            self.DL = self.G["DL"]
            for n in ("msh", "mp", "nmp"):
                self.G[n] = carve("g_" + n, [4, NS])
            self.G["e2m"] = carve("g_e2m", [4, 4, NS])
            self.Blast = carve("Blast", [4, 1])
            self.Mlast = carve("Mlast", [4, 1])
            self.ETM = carve("ETM", [128, NS, 12])
            self.E2L = carve("E2L", [128, 4, self.NCH + 1])
            self.PT = [carve("PT%d" % i, [128, 128], BF16) for i in range(4)]
            self.KH = [carve("KH%d" % i, [128, 128], BF16) for i in range(4)]
            self.CH = [carve("CH%d" % i, [128, 129]) for i in range(4)]
            self.CB = [carve("CB%d" % i, [128, 130], BF16) for i in range(4)]
            self.R = carve("R", [128, 4])
            self.R2 = carve("R2", [128, 4])
            self.HH = [carve("HH%d" % i, [128, 128]) for i in range(4)]
            self.HM = carve("HM", [128, 512], BF16)
        if self.n_odd:
            self._uoff = 0
            self.ngb = carve("ngb", [128, 4])
            self.w2g_st = carve("w2g_st", [16, 512])
            self.w2g = carve("w2g", [16, 512], BF16)
            self.GLB = carve("GLB", [16, TT], BF16)
            self.EBQ = carve("EBQ", [128, TT])
            self.EBK = carve("EBK", [128, TT])
            self.EA = [carve("EA%d" % i, [128, self.NCH + 1]) for i in range(4)]
            self.VG = [carve("VG%d" % i, [128, 1024], BF16) for i in range(NS)]
            self.GR = [carve("GR%d" % i, [128, 1024], BF16) for i in range(NS)]
            self.AT = carve("AT", [128, 512], BF16)
            self.KTM = carve("KTM", [128, 512], BF16)
            self.SH = [carve("SH%d" % i, [128, 256]) for i in range(4)]
            self.SB = [carve("SB%d" % i, [128, 256], BF16) for i in range(4)]
            self.OM = carve("OM", [128, 1024], BF16)
        P.memset(self.ones_f[:, :], 1.0, [self.ones_f])
        P.memset(self.ones_bf[:, :], 1.0, [self.ones_bf])
        P.memset(self.epsc[:, :], EPS, [self.epsc])
        P.memset(self.RM[:, :], 1.0, [self.RM])
        P.memset(self.RM[:, :].rearrange("p (c l) -> p c l", l=128)[:, :, 0:1], 0.0, [self.RM])
        P.op("pool", lambda e: e.affine_select(out=self.ident_f[:, :], in_=self.ones_f[:, :], pattern=[[1, 128]], compare_op=ALU.is_equal, fill=0.0, base=0, channel_multiplier=-1),
             [self.ones_f], [self.ident_f])
        P.cp(self.ident_b[:, :], self.ident_f[:, :], [self.ident_f], [self.ident_b], eng="pool")
        P.op("pool", lambda e: e.affine_select(out=self.CMASK[:, :], in_=self.ones_f[:, :], pattern=[[1, 128]], compare_op=ALU.is_ge, fill=0.0, base=0, channel_multiplier=-1),
             [self.ones_f], [self.CMASK])
        for h in range(4):
            P.cp(self.CMASK4[:, h * 128:(h + 1) * 128], self.CMASK[:, :], [self.CMASK], [self.CMASK4], eng="pool")
        P.cp(self.BM[:, :, :], self.ident_f[0:4, 0:4].unsqueeze(2).to_broadcast([4, 4, NS]), [self.ident_f], [self.BM], eng="pool")
        P.dma(self.ng[:, :], self.ngd[:, :], [self.ngd], [self.ng])
        for k in smalls:
            P.dma(getattr(self, k)[:, :], sm_d[k][:, :], [sm_d[k]], [getattr(self, k)])
        P.dma(self.gbi[:, :], gbi_d[:, :], [gbi_d], [self.gbi])
        P.dma(self.gbf[:, :], gbf_d[:, :], [gbf_d], [self.gbf])
        seq = [(l, tt) for l in range(depth) for tt in range(self.NT)]

        def load_resid(i):
            l, tt = seq[i]
            xT = self.xTs[i % 2]
            t0 = tt * TT
            tg = P.tag
            P.tag = "resid_io"
            if l == 0:
                for s in range(NS):
                    P.dma(self.xin[:, :], self.x[t0 + s * 128:t0 + (s + 1) * 128, :], [self.x], [self.xin])
                    for half in range(2):
                        ps = P.nps()
                        for q in range(4):
                            k = half * 4 + q
                            P.tr(ps[:, q * 128:(q + 1) * 128], self.xin[:, k * 128:(k + 1) * 128], self.ident_f[:, :], [self.xin, self.ident_f], [ps])
                        P.cp(xT[:, half * 4:half * 4 + 4, s * 128:(s + 1) * 128], ps[:, :].rearrange("p (k t) -> p k t", k=4), [ps], [xT], eng="dve")
            else:
                P.dma(xT[:, :, :], self.xres[:, t0:t0 + TT].rearrange("(k p) t -> p k t", p=128), [self.xres], [xT])
            P.tag = tg

        def norm1(i):
            l, tt = seq[i]
            P.tag = "norm1"
            self.rmsnorm(l * 2 * KC, self.hA, self.xTs[i % 2])

        def norm1_hooks(i):
            l, tt = seq[i]
            xT = self.xTs[i % 2]
            g0 = l * 2 * KC
            pbn = P.psb[7]
            sqs = [self.sq, self.sq2]
            hk = {}

            def add(j, f):
                hk.setdefault(j, []).append(f)

            def mk_sq(k):
                def f():
                    P.tag = "norm1"
                    P.tt(sqs[k % 2][:, :], xT[:, k, :], xT[:, k, :], ALU.mult, [xT], [sqs[k % 2]])
                return f

            def mk_mm(k):
                def f():
                    P.tag = "norm1"
                    P.mm(pbn[:, :], self.ones_bf[:, :], sqs[k % 2][:, :], k == 0, k == KC - 1, [sqs[k % 2], self.ones_bf], [pbn])
                return f

            def fin():
                P.tag = "norm1"
                P.act(self.rstd[:, :], pbn[:, :], AF.Sqrt, [pbn], [self.rstd], scale=1.0 / D, bias=self.epsc[:, 0:1])
                P.op("dve", lambda e: e.reciprocal(out=self.rstd[:, :], in_=self.rstd[:, :]), [self.rstd], [self.rstd])

            def mk_stt(k):
                def f():
                    P.tag = "norm1"
                    P.stt(self.hA[:, k, :], xT[:, k, :], self.ng[:, g0 + k:g0 + k + 1], self.rstd[:, :], ALU.mult, ALU.mult,
                          [xT, self.ng, self.rstd], [self.hA])
                return f

            for k in range(KC):
                add(2 + k, mk_sq(k))
                add(3 + k, mk_mm(k))
            add(11, fin)
            for k in range(KC):
                add(12 + k, mk_stt(k))
            return hk

        def norm1a(i):
            P.tag = "norm1"
            self.rms_a(self.xTs[i % 2])

        def norm1b(i):
            l, tt = seq[i]
            P.tag = "norm1"
            self.rms_b(l * 2 * KC, self.hA, self.xTs[i % 2])

        load_resid(0)
        norm1(0)
        for i, (l, tt) in enumerate(seq):
            even = (l % 2 == 0)
            li = l // 2
            t0 = tt * TT
            if tt == 0:
                self.barrier()
                if even:
                    for s_ in range(NS):
                        P.memset(self.VA[s_][:, :, :], 1.0, [self.VA[s_]])
                    P.dma(self.ehg[:, :], ehg_d[li], [ehg_d], [self.ehg])
                    self.even_layer_consts(li)
                else:
                    P.dma(self.ohg[:, :], ohg_d[li], [ohg_d], [self.ohg])
                    self.odd_layer_consts(li)
            self.cur_layer, self.cur_tt, self._blk = l, tt, 0
            self.xT = self.xTs[i % 2]
            self.hT = self.hA
            if even:
                self.even_mixer(li, tt)
                P.tag = "wout"
                self.proj_add(self.ev_w_out, self.ev_w_out[li], self.mixT, KC)
            else:
                self.odd_mixer(li, tt)
                P.tag = "wout"
                self.proj_add(self.od_w_out, self.od_w_out[li], self.mixT, KC)
            nxt = i + 1 < len(seq)
            if nxt:
                load_resid(i + 1)
            P.tag = "norm2"
            self.rmsnorm((l * 2 + 1) * KC, self.hF)
            self.hT = self.hF
            P.tag = "ffn13"
            self.ffn(l, hooks=norm1_hooks(i + 1) if nxt else None)
            P.tag = "resid_io"
            if l < depth - 1:
                P.dma(self.xres[:, t0:t0 + TT].rearrange("(k p) t -> p k t", p=128), self.xT[:, :, :], [self.xT], [self.xres])
            else:
                self.final_out(t0)
        P.op("sp", None, [self.out], [])
        P.emit()

    def barrier(self):
        P = self.P
        f = self.fence
        ps = P.nps()
        P.mm(ps[0:1, 0:2], self.ones_bf[:, 0:1], self.ones_bf[:, 0:2], True, True, [self.ones_bf], [ps])
        P.memset(f["dve"][:, :], 0.0, [f["dve"]], eng="dve")
        P.memset(f["pool"][:, :], 0.0, [f["pool"]], eng="pool")
        P.cp(f["act"][:, :], self.epsc[:, :], [self.epsc], [f["act"]])
        allf = [ps, f["dve"], f["pool"], f["act"]]
        for eng in ("pe", "act", "dve", "pool", "sp"):
            P.op(eng, None, allf, [])

    def final_out(self, t0):
        P = self.P
        self.rmsnorm(2 * self.depth * KC, self.yT)
        for s in range(NS):
            for half in range(2):
                ps = P.nps()
                for q in range(4):
                    k = half * 4 + q
                    P.tr(ps[:, q * 128:(q + 1) * 128], self.yT[:, k, s * 128:(s + 1) * 128], self.ident_f[:, :], [self.yT, self.ident_f], [ps])
                P.cp(self.yout[:, half * 512:(half + 1) * 512], ps[:, :], [ps], [self.yout], eng="dve")
            P.dma(self.out[t0 + s * 128:t0 + (s + 1) * 128, :], self.yout[:, :], [self.yout], [self.out])


def _cols(v):
    v = np.asarray(v, np.float32)
    return np.ascontiguousarray(v.reshape(-1, 128).T)


def prep_inputs(inp, depth, n_even, n_odd):
    f = lambda a: np.ascontiguousarray(np.asarray(a, np.float32))
    ng = [np.zeros((128, 0), np.float32)]
    for l in range(depth):
        ng.append(_cols(inp["mix_norm_g"][l]))
        ng.append(_cols(inp["ffn_norm_g"][l]))
    ng.append(_cols(inp["final_norm_g"]))
    m = {"ng": np.ascontiguousarray(np.concatenate(ng, 1))}
    m["ffn_w1"] = f(inp["ffn_w1"][:depth])
    m["ffn_w3"] = f(inp["ffn_w3"][:depth])
    m["ffn_w2"] = f(inp["ffn_w2"][:depth])
    m["ev_w_in"] = f(inp["ev_w_in"][:n_even])
    m["ev_w_out"] = f(inp["ev_w_out"][:n_even])
    no = max(n_odd, 1)
    m["od_w_in"] = f(inp["od_w_in"][:no])
    m["od_w_out"] = f(inp["od_w_out"][:no])
    m["od_gate_w2"] = f(inp["od_gate_w2"][:no])
    cw = np.asarray(inp["ev_conv_w"], np.float32)[:n_even]
    m["s_cw"] = np.ascontiguousarray(cw.reshape(n_even, 31, 4, 128).transpose(3, 0, 2, 1).reshape(128, -1))
    colsE = lambda a, n: np.ascontiguousarray(np.asarray(a, np.float32)[:n_even].reshape(n_even, n, 128).transpose(2, 0, 1).reshape(128, -1))
    m["s_cb"] = colsE(inp["ev_conv_b"], 4)
    m["s_lng"] = colsE(inp["ev_ln_g"], 4)
    m["s_lnb"] = colsE(inp["ev_ln_b"], 4)
    qw = np.asarray(inp["ev_qk_conv_w"], np.float32)[:n_even]
    m["s_qw"] = np.ascontiguousarray(qw.reshape(n_even, 4, 8, 128).transpose(3, 0, 2, 1).reshape(128, -1))
    m["s_qb"] = colsE(inp["ev_qk_conv_b"], 8)
    gb = np.asarray(inp["ev_gate_b"], np.float32)[:n_even]
    m["s_gbi"] = np.ascontiguousarray(gb[:, 0:4].T)
    m["s_gbf"] = np.ascontiguousarray(gb[:, 4:8].T)
    ogb = np.asarray(inp["od_gate_b"], np.float32)[:no]
    m["s_ogb"] = np.ascontiguousarray(ogb.reshape(no, 4, 128).transpose(2, 0, 1).reshape(128, -1))
    ehg = np.asarray(inp["ev_head_g"], np.float32)[:n_even]
    m["s_ehg"] = np.ascontiguousarray(np.broadcast_to(ehg[:, None, :], (n_even, 128, 512)))
    ohg = np.asarray(inp["od_head_g"], np.float32)[:no]
    m["s_ohg"] = np.ascontiguousarray(np.broadcast_to(ohg[:, None, :], (no, 128, 1024)))
    return m


_CACHE = {}


def run(inp, T_len, depth, ncores):
    key = (T_len, depth)
    if key not in _CACHE:
        _CACHE[key] = Builder(T_len, depth)
    b = _CACHE[key]
    shared = prep_inputs(inp, depth, b.n_even, b.n_odd)
    x = np.asarray(inp["x"], np.float32)
    in_maps = []
    for c in range(ncores):
        mm = dict(shared)
        mm["x"] = np.ascontiguousarray(x[c, :T_len])
        in_maps.append(mm)
    res = run_bass_kernel_spmd(b.nc, in_maps, core_ids=list(range(ncores)))
    return np.stack([r["out"] for r in res.results], 0)


def kernel(**inputs):
    return run(inputs, 4096, 4, 8).astype(np.float32)
```
